# Optimizing a Trainium2 kernel written in Bass

```python
import math
import jax
import jax.numpy as jnp
from jax import lax

D_MODEL = 2048
BATCH = 8
SEQ = 2048
DEPTH = 2
DEC_BATCH = 16
DEC_SEQ = 64
PAST_LEN = 4096

CHUNK = 64
N_EVEN = (DEPTH + 1) // 2
N_ODD = DEPTH // 2

SB_HEAD_DIM = 64
SB_HEADS = D_MODEL // 128
SB_WIDTH = SB_HEADS * SB_HEAD_DIM
SB_QBLOCK = 128

SSD_HEAD_DIM = 64
SSD_WIDTH = D_MODEL
SSD_HEADS = SSD_WIDTH // SSD_HEAD_DIM
SSD_GROUPS = 4
SSD_STATE = 128
SSD_CONV = 4
SSD_CONV_DIM = SSD_WIDTH + 2 * SSD_GROUPS * SSD_STATE
SSD_CHUNK = 64

CA_HEAD_DIM = 64
CA_WIDTH = D_MODEL
CA_HEADS = CA_WIDTH // CA_HEAD_DIM
CA_LEFT_CHUNKS = 8
CA_REL_CLIP = 128

DEEPNORM_ALPHA = (2 * DEPTH) ** 0.25
DEEPNORM_BETA = (8 * DEPTH) ** -0.25
LN_EPS = 1e-5
RMS_EPS = 1e-5

L0_SIZES = (SB_WIDTH, SB_WIDTH, SB_WIDTH, SB_WIDTH, SSD_WIDTH, SSD_CONV_DIM, SSD_HEADS)
L0_SPLITS = tuple(sum(L0_SIZES[:i + 1]) for i in range(len(L0_SIZES) - 1))
L0_IN = sum(L0_SIZES)
L0_MIX = SB_WIDTH + SSD_WIDTH
L1_IN = 4 * CA_WIDTH

kernel_name = 'hybrid_streaming_encoder_step'


def layer_norm(x, g, b):
    xf = x.astype(jnp.float32)
    mu = jnp.mean(xf, axis=-1, keepdims=True)
    var = jnp.mean(jnp.square(xf - mu), axis=-1, keepdims=True)
    return ((xf - mu) * lax.rsqrt(var + LN_EPS) * g + b).astype(x.dtype)


def _sb_block(q, k, v, q_pos0):
    z = jnp.einsum('bqhd,bkhd->bhqk', q, k).astype(jnp.float32) * (SB_HEAD_DIM ** -0.5)
    q_pos = q_pos0 + jnp.arange(q.shape[1])
    k_pos = jnp.arange(k.shape[1])
    strict = k_pos[None, :] < q_pos[:, None]
    sp = jnp.where(strict, jax.nn.softplus(z), 0.0)
    later = lax.cumsum(sp, axis=3, reverse=True) - sp
    w = jnp.where(strict, jnp.exp(jax.nn.log_sigmoid(z) - later), 0.0)
    return jnp.einsum('bhqk,bkhd->bqhd', w.astype(v.dtype), v)


def stick_breaking(q, k, v, q_pos0):
    tq = q.shape[1]
    outs = []
    for s in range(0, tq, SB_QBLOCK):
        e = min(s + SB_QBLOCK, tq)
        outs.append(_sb_block(q[:, s:e], k[:, :q_pos0 + e], v[:, :q_pos0 + e], q_pos0 + s))
    return jnp.concatenate(outs, axis=1)


def causal_conv(xbc, conv_state, conv_w, conv_b):
    L = xbc.shape[1]
    xp = jnp.concatenate([conv_state.astype(xbc.dtype), xbc], axis=1)
    y = conv_b + conv_w[0] * xp[:, 0:L]
    for t in range(1, SSD_CONV):
        y = y + conv_w[t] * xp[:, t:t + L]
    return jax.nn.silu(y), xp[:, -(SSD_CONV - 1):]


def ssd_scan(x, dt, a, bm, cm, h0):
    b_, L = x.shape[:2]
    q_len = min(SSD_CHUNK, L)
    nc = L // q_len
    r = SSD_HEADS // SSD_GROUPS
    f32 = jnp.float32
    xc = x.astype(f32).reshape(b_, nc, q_len, SSD_GROUPS, r, SSD_HEAD_DIM)
    dtc = dt.reshape(b_, nc, q_len, SSD_GROUPS, r)
    bc = bm.astype(f32).reshape(b_, nc, q_len, SSD_GROUPS, SSD_STATE)
    cc = cm.astype(f32).reshape(b_, nc, q_len, SSD_GROUPS, SSD_STATE)
    cum = jnp.cumsum(dtc * a.reshape(SSD_GROUPS, r), axis=2)
    seg = cum[:, :, :, None] - cum[:, :, None, :]
    tril = jnp.tril(jnp.ones((q_len, q_len), dtype=bool))[:, :, None, None]
    decay = jnp.exp(jnp.where(tril, seg, -jnp.inf))
    cb = jnp.einsum('bctgn,bcsgn->bctsg', cc, bc)
    w = cb[..., None] * decay * dtc[:, :, None]
    y_diag = jnp.einsum('bctsgr,bcsgrp->bctgrp', w, xc)
    to_end = jnp.exp(cum[:, :, -1:] - cum) * dtc
    states = jnp.einsum('bcsgn,bcsgr,bcsgrp->bcgrpn', bc, to_end, xc)
    chunk_decay = jnp.exp(cum[:, :, -1])

    def step(h, inp):
        st, dec = inp
        return dec[..., None, None] * h + st, h

    h_init = h0.astype(f32).reshape(b_, SSD_GROUPS, r, SSD_HEAD_DIM, SSD_STATE)
    h_final, h_in = lax.scan(step, h_init, (jnp.moveaxis(states, 1, 0), jnp.moveaxis(chunk_decay, 1, 0)))
    h_in = jnp.moveaxis(h_in, 0, 1)
    y_off = jnp.einsum('bctgn,bcgrpn,bctgr->bctgrp', cc, h_in, jnp.exp(cum))
    y = (y_diag + y_off).reshape(b_, L, SSD_HEADS, SSD_HEAD_DIM)
    return y, h_final.reshape(b_, SSD_HEADS, SSD_HEAD_DIM, SSD_STATE)


def ssd_branch(z, xbc, dt_raw, conv_state, h0, conv_w, conv_b, dt_bias, a_log, d_skip, norm_w):
    b_, L = z.shape[:2]
    gn = SSD_GROUPS * SSD_STATE
    xbc_act, new_conv = causal_conv(xbc, conv_state, conv_w, conv_b)
    xs = xbc_act[..., :SSD_WIDTH].reshape(b_, L, SSD_HEADS, SSD_HEAD_DIM)
    bm = xbc_act[..., SSD_WIDTH:SSD_WIDTH + gn].reshape(b_, L, SSD_GROUPS, SSD_STATE)
    cm = xbc_act[..., SSD_WIDTH + gn:].reshape(b_, L, SSD_GROUPS, SSD_STATE)
    dt = jax.nn.softplus(dt_raw.astype(jnp.float32) + dt_bias.astype(jnp.float32))
    a = -jnp.exp(a_log.astype(jnp.float32))
    y, new_h = ssd_scan(xs, dt, a, bm, cm, h0)
    y = y + d_skip.astype(jnp.float32)[:, None] * xs.astype(jnp.float32)
    y = y.reshape(b_, L, SSD_WIDTH) * jax.nn.silu(z.astype(jnp.float32))
    yg = y.reshape(b_, L, SSD_GROUPS, SSD_WIDTH // SSD_GROUPS)
    yg = yg * lax.rsqrt(jnp.mean(yg * yg, axis=-1, keepdims=True) + RMS_EPS)
    y = yg.reshape(b_, L, SSD_WIDTH) * norm_w
    return y.astype(z.dtype), new_conv, new_h


def even_layer(x, past_k, past_v, conv_state, h0, w_in, conv_w, conv_b, dt_bias, a_log, d_skip,
               norm_w, w_out, ln_g, ln_b):
    b_, L = x.shape[:2]
    proj = jnp.einsum('bld,de->ble', x, w_in)
    q, k, v, g_a, z, xbc, dt_raw = jnp.split(proj, L0_SPLITS, axis=-1)
    q = q.reshape(b_, L, SB_HEADS, SB_HEAD_DIM)
    k = k.reshape(b_, L, SB_HEADS, SB_HEAD_DIM)
    v = v.reshape(b_, L, SB_HEADS, SB_HEAD_DIM)
    k_all = jnp.concatenate([past_k.astype(k.dtype), k], axis=1)
    v_all = jnp.concatenate([past_v.astype(v.dtype), v], axis=1)
    o_a = stick_breaking(q, k_all, v_all, past_k.shape[1]).reshape(b_, L, SB_WIDTH) * jax.nn.silu(g_a)
    o_b, new_conv, new_h = ssd_branch(z, xbc, dt_raw, conv_state, h0, conv_w, conv_b, dt_bias, a_log,
                                      d_skip, norm_w)
    mix = jnp.einsum('ble,ed->bld', jnp.concatenate([o_a, o_b], axis=-1), w_out)
    y = layer_norm(DEEPNORM_ALPHA * x + mix, ln_g, ln_b)
    return y, k, v, new_conv, new_h


def chunk_band_attention(q, k, v, past_k, past_v, rel_bias):
    b_, L, H, d = q.shape
    P = past_k.shape[1]
    left = CA_LEFT_CHUNKS * CHUNK
    qn = min(CHUNK, L)
    n_blocks = L // qn
    band = left + qn
    pad = jnp.zeros((b_, left - P, H, d), k.dtype)
    kp = jnp.concatenate([pad, past_k.astype(k.dtype), k], axis=1)
    vp = jnp.concatenate([pad, past_v.astype(v.dtype), v], axis=1)
    valid = jnp.arange(left + L) >= left - P
    qi = jnp.arange(qn)[:, None]
    kj = jnp.arange(band)[None, :]
    rel = jnp.clip(qi - kj + left, -CA_REL_CLIP, CA_REL_CLIP) + CA_REL_CLIP
    bias = rel_bias[:, rel].astype(jnp.float32)

    def one_block(c):
        start = c * qn
        qc = lax.dynamic_slice_in_dim(q, start, qn, axis=1)
        kc = lax.dynamic_slice_in_dim(kp, start, band, axis=1)
        vc = lax.dynamic_slice_in_dim(vp, start, band, axis=1)
        mc = lax.dynamic_slice_in_dim(valid, start, band, axis=0)
        s = jnp.einsum('bqhd,bkhd->bhqk', qc, kc).astype(jnp.float32) * (CA_HEAD_DIM ** -0.5) + bias
        s = jnp.where(mc, s, -jnp.inf)
        p = jax.nn.softmax(s, axis=-1)
        return jnp.einsum('bhqk,bkhd->bqhd', p.astype(vc.dtype), vc)

    out = lax.map(one_block, jnp.arange(n_blocks))
    return jnp.moveaxis(out, 0, 1).reshape(b_, L, H, d)


def odd_layer(x, past_k, past_v, w_in, rel_bias, w_out, ln_g, ln_b):
    b_, L = x.shape[:2]
    proj = jnp.einsum('bld,de->ble', x, w_in)
    q, k, v, g = jnp.split(proj, 4, axis=-1)
    q = q.reshape(b_, L, CA_HEADS, CA_HEAD_DIM)
    k = k.reshape(b_, L, CA_HEADS, CA_HEAD_DIM)
    v = v.reshape(b_, L, CA_HEADS, CA_HEAD_DIM)
    o = chunk_band_attention(q, k, v, past_k, past_v, rel_bias).reshape(b_, L, CA_WIDTH) * jax.nn.silu(g)
    mix = jnp.einsum('ble,ed->bld', o, w_out)
    y = layer_norm(DEEPNORM_ALPHA * x + mix, ln_g, ln_b)
    k_cat = jnp.concatenate([past_k.astype(k.dtype), k], axis=1)
    v_cat = jnp.concatenate([past_v.astype(v.dtype), v], axis=1)
    keep = min(CA_LEFT_CHUNKS * CHUNK, k_cat.shape[1])
    return y, k_cat[:, -keep:], v_cat[:, -keep:]


def setup_inputs(seed: int = 0) -> dict:
    key = jax.random.key(seed)
    ks = jax.random.split(key, 24)
    f32 = jnp.float32
    nrm = jax.random.normal
    band_rows = min(CA_LEFT_CHUNKS * CHUNK, PAST_LEN)
    x_prompt = nrm(ks[0], (BATCH, SEQ, D_MODEL), f32)
    x_sample = nrm(ks[1], (DEC_BATCH, DEC_SEQ, D_MODEL), f32)
    cache_sb_k = nrm(ks[2], (N_EVEN, DEC_BATCH, PAST_LEN, SB_HEADS, SB_HEAD_DIM), f32)
    cache_sb_v = 0.5 * nrm(ks[3], (N_EVEN, DEC_BATCH, PAST_LEN, SB_HEADS, SB_HEAD_DIM), f32)
    state_ssm = 0.1 * nrm(ks[4], (N_EVEN, DEC_BATCH, SSD_HEADS, SSD_HEAD_DIM, SSD_STATE), f32)
    state_conv = nrm(ks[5], (N_EVEN, DEC_BATCH, SSD_CONV - 1, SSD_CONV_DIM), f32)
    cache_band_k = nrm(ks[6], (N_ODD, DEC_BATCH, band_rows, CA_HEADS, CA_HEAD_DIM), f32)
    cache_band_v = 0.5 * nrm(ks[7], (N_ODD, DEC_BATCH, band_rows, CA_HEADS, CA_HEAD_DIM), f32)
    col0 = jnp.concatenate([jnp.ones((2 * SB_WIDTH,), f32), jnp.full((SB_WIDTH,), DEEPNORM_BETA, f32),
                            jnp.ones((L0_IN - 3 * SB_WIDTH,), f32)])
    even_w_in = nrm(ks[8], (N_EVEN, D_MODEL, L0_IN), f32) * (D_MODEL ** -0.5) * col0
    even_conv_w = 0.5 * nrm(ks[9], (N_EVEN, SSD_CONV, SSD_CONV_DIM), f32)
    even_conv_b = 0.1 * nrm(ks[10], (N_EVEN, SSD_CONV_DIM), f32)
    u = jax.random.uniform(ks[11], (N_EVEN, SSD_HEADS), f32)
    dt0 = jnp.exp(u * (math.log(0.1) - math.log(0.001)) + math.log(0.001))
    even_dt_bias = dt0 + jnp.log(-jnp.expm1(-dt0))
    even_a_log = jnp.log(jax.random.uniform(ks[12], (N_EVEN, SSD_HEADS), f32, 1.0, 16.0))
    even_d_skip = 1.0 + 0.1 * nrm(ks[13], (N_EVEN, SSD_HEADS), f32)
    even_norm_w = 1.0 + 0.1 * nrm(ks[14], (N_EVEN, SSD_WIDTH), f32)
    even_w_out = nrm(ks[15], (N_EVEN, L0_MIX, D_MODEL), f32) * (L0_MIX ** -0.5) * DEEPNORM_BETA
    even_ln_g = 1.0 + 0.1 * nrm(ks[16], (N_EVEN, D_MODEL), f32)
    even_ln_b = 0.1 * nrm(ks[17], (N_EVEN, D_MODEL), f32)
    col1 = jnp.concatenate([jnp.ones((2 * CA_WIDTH,), f32), jnp.full((CA_WIDTH,), DEEPNORM_BETA, f32),
                            jnp.ones((CA_WIDTH,), f32)])
    odd_w_in = nrm(ks[18], (N_ODD, D_MODEL, L1_IN), f32) * (D_MODEL ** -0.5) * col1
    odd_rel_bias = 0.5 * nrm(ks[19], (N_ODD, CA_HEADS, 2 * CA_REL_CLIP + 1), f32)
    odd_w_out = nrm(ks[20], (N_ODD, CA_WIDTH, D_MODEL), f32) * (CA_WIDTH ** -0.5) * DEEPNORM_BETA
    odd_ln_g = 1.0 + 0.1 * nrm(ks[21], (N_ODD, D_MODEL), f32)
    odd_ln_b = 0.1 * nrm(ks[22], (N_ODD, D_MODEL), f32)
    return {'x_prompt': x_prompt, 'x_sample': x_sample,
            'cache_sb_k': cache_sb_k, 'cache_sb_v': cache_sb_v,
            'state_ssm': state_ssm, 'state_conv': state_conv,
            'cache_band_k': cache_band_k, 'cache_band_v': cache_band_v,
            'even_w_in': even_w_in, 'even_conv_w': even_conv_w, 'even_conv_b': even_conv_b,
            'even_dt_bias': even_dt_bias, 'even_a_log': even_a_log, 'even_d_skip': even_d_skip,
            'even_norm_w': even_norm_w, 'even_w_out': even_w_out, 'even_ln_g': even_ln_g,
            'even_ln_b': even_ln_b,
            'odd_w_in': odd_w_in, 'odd_rel_bias': odd_rel_bias, 'odd_w_out': odd_w_out,
            'odd_ln_g': odd_ln_g, 'odd_ln_b': odd_ln_b}


def reference(x_prompt, x_sample, cache_sb_k, cache_sb_v, state_ssm, state_conv, cache_band_k,
              cache_band_v, even_w_in, even_conv_w, even_conv_b, even_dt_bias, even_a_log, even_d_skip,
              even_norm_w, even_w_out, even_ln_g, even_ln_b, odd_w_in, odd_rel_bias, odd_w_out,
              odd_ln_g, odd_ln_b):
    bp = x_prompt.shape[0]
    yp, ys = x_prompt, x_sample
    p_sb_k, p_sb_v, p_ssm, p_conv, p_band_k, p_band_v = [], [], [], [], [], []
    s_sb_k, s_sb_v, s_ssm, s_conv, s_band_k, s_band_v = [], [], [], [], [], []
    for layer in range(DEPTH):
        i = layer // 2
        if layer % 2 == 0:
            ew = (even_w_in[i], even_conv_w[i], even_conv_b[i], even_dt_bias[i], even_a_log[i],
                  even_d_skip[i], even_norm_w[i], even_w_out[i], even_ln_g[i], even_ln_b[i])
            no_past = jnp.zeros((bp, 0, SB_HEADS, SB_HEAD_DIM), yp.dtype)
            zero_conv = jnp.zeros((bp, SSD_CONV - 1, SSD_CONV_DIM), yp.dtype)
            zero_h = jnp.zeros((bp, SSD_HEADS, SSD_HEAD_DIM, SSD_STATE), jnp.float32)
            yp, k_new, v_new, conv_new, h_new = even_layer(yp, no_past, no_past, zero_conv, zero_h, *ew)
            p_sb_k.append(k_new)
            p_sb_v.append(v_new)
            p_conv.append(conv_new)
            p_ssm.append(h_new)
            ys, k_new, v_new, conv_new, h_new = even_layer(ys, cache_sb_k[i], cache_sb_v[i], state_conv[i],
                                                           state_ssm[i], *ew)
            s_sb_k.append(k_new)
            s_sb_v.append(v_new)
            s_conv.append(conv_new)
            s_ssm.append(h_new)
        else:
            ow = (odd_w_in[i], odd_rel_bias[i], odd_w_out[i], odd_ln_g[i], odd_ln_b[i])
            no_past = jnp.zeros((bp, 0, CA_HEADS, CA_HEAD_DIM), yp.dtype)
            yp, bk, bv = odd_layer(yp, no_past, no_past, *ow)
            p_band_k.append(bk)
            p_band_v.append(bv)
            ys, bk, bv = odd_layer(ys, cache_band_k[i], cache_band_v[i], *ow)
            s_band_k.append(bk)
            s_band_v.append(bv)
    return (yp, ys,
            jnp.stack(p_sb_k), jnp.stack(p_sb_v), jnp.stack(p_ssm), jnp.stack(p_conv),
            jnp.stack(p_band_k), jnp.stack(p_band_v),
            jnp.stack(s_sb_k), jnp.stack(s_sb_v), jnp.stack(s_ssm), jnp.stack(s_conv),
            jnp.stack(s_band_k), jnp.stack(s_band_v))
```

```python
from contextlib import ExitStack
import numpy as np
import concourse.bass as bass
import concourse.mybir as mybir
from concourse.bass_utils import run_bass_kernel_spmd

F32 = mybir.dt.float32
BF16 = mybir.dt.bfloat16
F32R = mybir.dt.float32r
AF = mybir.ActivationFunctionType
ALU = mybir.AluOpType
AX = mybir.AxisListType

NCORES = 8
D = 2048
KC = 16
SEQ = 2048
NT = 17
TOK = NT * 128
L0_IN = 9248
SEG = 30000


class _Scope:
    def __init__(self, S):
        self.S = S

    def __enter__(self):
        self.prev = self.S.stack
        self.st = ExitStack()
        self.S.stack = self.st
        return self

    def __exit__(self, *a):
        self.S.barrier()
        self.st.close()
        self.S.stack = self.prev
        return False


class Region:
    __slots__ = ("w", "r")

    def __init__(self):
        self.w = None
        self.r = {}


class Ring:
    def __init__(self, tiles):
        self.t = tiles
        self.R = [Region() for _ in tiles]
        self.i = -1

    def next(self):
        self.i = (self.i + 1) % len(self.t)
        return self.t[self.i], self.R[self.i]


class Sched:
    ENGS = ("pe", "act", "dve", "pool", "sp")

    def __init__(self, nc, stack, dma_slots=8):
        self.nc = nc
        self.root = stack
        self.stack = stack
        self.e = {"pe": nc.tensor, "act": nc.scalar, "dve": nc.vector,
                  "pool": nc.gpsimd, "sp": nc.sync}
        self.sems = {}
        self.cnt = {k: 0 for k in self.ENGS}
        self.waited = {k: {} for k in self.ENGS}
        self.dma_slots = dma_slots
        self.dq = {}
        self.ninst = 0
        self.nwait = 0
        self._n = 0

    def sbuf(self, shape, dt, name=None):
        self._n += 1
        return self.stack.enter_context(
            self.nc.sbuf_tensor("%s_%d" % (name or "sb", self._n), list(shape), dt))

    def psum(self, shape, dt, name=None):
        self._n += 1
        return self.stack.enter_context(
            self.nc.psum_tensor("%s_%d" % (name or "ps", self._n), list(shape), dt))

    def dram(self, name, shape, dt, kind="Internal"):
        return self.nc.dram_tensor(name, list(shape), dt, kind=kind).ap()

    def _sem(self, key):
        s = self.sems.get(key)
        if s is None:
            s = self.root.enter_context(self.nc.semaphore("s_" + key.replace("#", "_")))
            self.sems[key] = s
        return s

    def _emit_waits(self, eng, need):
        w = self.waited[eng]
        for key, val in need.items():
            if val <= 0 or w.get(key, 0) >= val:
                continue
            if eng == "pe" and key.startswith("pe#"):
                continue
            self.e[eng].wait_ge(self._sem(key), val)
            self.nwait += 1
            w[key] = val

    @staticmethod
    def _need(reads, writes):
        need = {}
        for R in reads:
            if R.w is not None:
                k, v = R.w
                if need.get(k, 0) < v:
                    need[k] = v
        for R in writes:
            if R.w is not None:
                k, v = R.w
                if need.get(k, 0) < v:
                    need[k] = v
            for k, v in R.r.items():
                if need.get(k, 0) < v:
                    need[k] = v
        return need

    @staticmethod
    def _mark(tok, reads, writes):
        k, v = tok
        for R in reads:
            if R.r.get(k, 0) < v:
                R.r[k] = v
        for R in writes:
            R.w = tok
            R.r = {}

    def op(self, eng, fn, reads=(), writes=()):
        self._emit_waits(eng, self._need(reads, writes))
        inst = fn(self.e[eng])
        c = self.cnt[eng]
        self.cnt[eng] = c + 1
        key = "%s#%d" % (eng, c // SEG)
        inst.then_inc(self._sem(key), 1)
        self._mark((key, c % SEG + 1), reads, writes)
        self.ninst += 1
        return inst

    def dma(self, queue, out, in_, reads=(), writes=(), **kw):
        q = self.dq.get(queue)
        if q is None:
            q = {"n": 0, "uses": [0] * self.dma_slots,
                 "keys": ["d%s%d" % (queue, i) for i in range(self.dma_slots)]}
            self.dq[queue] = q
        slot = q["n"] % self.dma_slots
        q["n"] += 1
        key = q["keys"][slot]
        need = self._need(reads, writes)
        prev = 16 * q["uses"][slot]
        if prev > 0 and need.get(key, 0) < prev:
            need[key] = prev
        w = self.waited[queue]
        for k, v in need.items():
            if v <= 0 or w.get(k, 0) >= v:
                continue
            self.e[queue].wait_ge(self._sem(k), v)
            self.nwait += 1
            w[k] = v
        inst = self.e[queue].dma_start(out=out, in_=in_, **kw)
        q["uses"][slot] += 1
        inst.then_inc(self._sem(key), 16)
        self._mark((key, 16 * q["uses"][slot]), reads, writes)
        self.ninst += 1
        return inst

    def scope(self):
        return _Scope(self)

    def _all_tokens(self):
        need = {}
        for k in self.ENGS:
            c = self.cnt[k]
            if c > 0:
                need["%s#%d" % (k, (c - 1) // SEG)] = (c - 1) % SEG + 1
        for q in self.dq.values():
            for i, key in enumerate(q["keys"]):
                if q["uses"][i]:
                    need[key] = 16 * q["uses"][i]
        return need

    def barrier(self, engines=None):
        need = self._all_tokens()
        for eng in (engines or self.ENGS):
            w = self.waited[eng]
            for k, v in need.items():
                if w.get(k, 0) >= v:
                    continue
                self.e[eng].wait_ge(self._sem(k), v)
                self.nwait += 1
                w[k] = v


DEBUG = False
SKIP_SB = False
SKIP_REST = False
TG = [(0, 512), (512, 512), (1024, 512), (1536, 512), (2048, 128)]


def build_program():
    nc = bass.Bass("TRN2", target_bir_lowering=False)
    st = ExitStack()
    S = Sched(nc, st)

    def din(name, shape):
        return nc.dram_tensor(name, list(shape), F32, kind="ExternalInput").ap()

    def dout(name, shape, dt=F32):
        return nc.dram_tensor(name, list(shape), dt, kind="ExternalOutput").ap()

    def dscr(name, shape, dt):
        kind = "ExternalOutput" if DEBUG else "Internal"
        return nc.dram_tensor(name, list(shape), dt, kind=kind).ap()

    xp = din("xp", [SEQ, D])
    xs = din("xs", [128, D])
    csk = din("csk", [2, 4096, 1024])
    csv = din("csv", [2, 4096, 1024])
    w_in0 = din("w_in0", [D, L0_IN])
    sssm = din("sssm", [2, 2048, 128])
    sconv = din("sconv", [2, 3, 3072])
    conv_w = din("conv_w", [4, 3072])
    conv_b = din("conv_b", [3072])
    dt_bias = din("dt_bias", [32])
    a_log = din("a_log", [32])
    d_skip = din("d_skip", [32])
    norm_w = din("norm_w", [2048])
    w_out0 = din("w_out0", [3072, 2048])
    ln_g0 = din("ln_g0", [2048])
    ln_b0 = din("ln_b0", [2048])
    cbk = din("cbk", [2, 512, 2048])
    cbv = din("cbv", [2, 512, 2048])
    w_in1 = din("w_in1", [D, 8192])
    rel_bias = din("rel_bias", [32, 257])
    w_out1 = din("w_out1", [2048, 2048])
    ln_g1 = din("ln_g1", [2048])
    ln_b1 = din("ln_b1", [2048])
    o_sbk_p = dout("o_sbk_p", [SEQ, 1024])
    o_sbv_p = dout("o_sbv_p", [SEQ, 1024])
    o_sbk_s = dout("o_sbk_s", [128, 1024])
    o_sbv_s = dout("o_sbv_s", [128, 1024])
    o_conv = dout("o_conv", [9, 3072])
    o_ssm = dout("o_ssm", [3, 2048, 128])
    o_yp = dout("o_yp", [SEQ, D])
    o_ys = dout("o_ys", [128, D])
    o_bk_p = dout("o_bk_p", [512, 2048])
    o_bv_p = dout("o_bv_p", [512, 2048])
    o_bk_s = dout("o_bk_s", [2, 512, 2048])
    o_bv_s = dout("o_bv_s", [2, 512, 2048])
    oT0 = dscr("oT0", [24, 128, TOK], BF16)
    oT0_R = [Region() for _ in range(24)]

    constR = Region()
    ident_b = S.sbuf([128, 128], BF16, "ident_b")
    ident_f = S.sbuf([128, 128], F32, "ident_f")
    ones_b = S.sbuf([128, 512], BF16, "ones_b")
    ones_f = S.sbuf([128, 128], F32, "ones_f")
    uincl = S.sbuf([128, 128], BF16, "uincl")
    S.op("pool", lambda e: e.memset(ones_b[:], 1.0), writes=[constR])
    zeros_f = S.sbuf([128, 256], F32, "zeros_f")
    ones_src = S.sbuf([128, 128], F32, "ones_src")
    S.op("pool", lambda e: e.memset(zeros_f[:], 0.0), writes=[constR])
    S.op("pool", lambda e: e.memset(ones_src[:], 1.0), writes=[constR])
    S.op("pool", lambda e: e.tensor_copy(out=ones_f[:].bitcast(F32R), in_=ones_src[:]),
         writes=[constR])
    S.op("pool", lambda e: e.memset(ident_b[:], 0.0), writes=[constR])
    S.op("pool", lambda e: e.memset(ident_f[:], 0.0), writes=[constR])
    for idt in (ident_b, ident_f):
        S.op("pool", lambda e: e.affine_select(
            out=idt[:], in_=idt[:], pattern=[[-1, 128]], compare_op=ALU.not_equal,
            fill=1.0, base=0, channel_multiplier=1), writes=[constR])
    S.op("pool", lambda e: e.affine_select(
        out=uincl[:], in_=ones_b[:, 0:128], pattern=[[-1, 128]], compare_op=ALU.is_ge,
        fill=0.0, base=0, channel_multiplier=1), writes=[constR])

    class _XT:
        def __init__(self):
            self.t = None
        def __getitem__(self, k):
            return self.t[k]
    xT = _XT()
    xT_R = [Region() for _ in range(NT)]

    def x_rows(src_p, src_s, t):
        return src_p[t * 128:(t + 1) * 128, :] if t < 16 else src_s[:, :]

    def build_xT(src_p, src_s, srcR=None):
        with S.scope():
            xb = Ring([S.sbuf([128, D], BF16, "xb%d" % i) for i in range(2)])
            pT = Ring([S.psum([128, 8, 128], BF16, "pT%d" % i) for i in range(2)])
            for t in range(NT):
                xbt, xbR = xb.next()
                S.dma("pool", xbt[:], x_rows(src_p, src_s, t), reads=([srcR] if srcR else []),
                      writes=[xbR])
                for g in range(2):
                    pt, pR = pT.next()
                    for j in range(8):
                        kc = g * 8 + j
                        S.op("pe", lambda e: e.transpose(
                            pt[:, j, :], xbt[:, kc * 128:(kc + 1) * 128], ident_b[:]),
                            reads=[xbR, constR], writes=[pR])
                    dst = xT[:, g * 8:(g + 1) * 8, t * 128:(t + 1) * 128]
                    if g == 0:
                        S.op("act", lambda e: e.copy(out=dst, in_=pt[:]), reads=[pR],
                             writes=[xT_R[t]])
                    else:
                        S.op("dve", lambda e: e.tensor_copy(out=dst, in_=pt[:]), reads=[pR],
                             writes=[xT_R[t]])

    def sb_pipeline(R, tiles):
        ss_prev = {}

        def S0(T):
            blk = T["blk"]
            nk = blk["nk"]
            zt, ztR = R["zt"].next()
            for sg, (kap, vap, kvR) in zip(T["segs"], blk["kv"]):
                S.op("pe", lambda e: e.matmul(zt[0:nk, sg["c0"]:sg["c0"] + sg["n"]], lhsT=kap,
                                              rhs=sg["qs"], start=True, stop=True),
                     reads=[kvR, sg["qR"]], writes=[ztR])
            T["zt"] = (zt, ztR)

        def S1(T):
            blk = T["blk"]
            nk, mask, ncols = blk["nk"], blk["mask"], T["ncols"]
            zt, ztR = T["zt"]
            et, etR = R["e"].next()
            S.op("act", lambda e: e.activation(out=et[0:nk, 0:ncols], in_=zt[0:nk, 0:ncols],
                                               func=AF.Exp), reads=[ztR], writes=[etR])
            spt, spR = R["sp"].next()
            S.op("act", lambda e: e.activation(out=spt[0:nk, 0:ncols], in_=et[0:nk, 0:ncols],
                                               func=AF.Ln, bias=1.0, scale=1.0),
                 reads=[etR], writes=[spR])
            if mask is not None:
                S.op("dve", lambda e: e.tensor_tensor(out=spt[0:nk, 0:ncols],
                                                      in0=spt[0:nk, 0:ncols], in1=mask,
                                                      op=ALU.mult),
                     reads=[spR, constR], writes=[spR])
            prev = ss_prev.get(T["chain"])
            T["sp"] = (spt, spR)
            T["ss_in"] = prev
            if not T["last"]:
                nst, nsR = R["ss"].next()
                if prev is None:
                    S.op("pool", lambda e: e.tensor_copy(out=nst[0:nk, 0:ncols].bitcast(F32R),
                                                         in_=spt[0:nk, 0:ncols]),
                         reads=[spR], writes=[nsR])
                else:
                    sst, ssR = prev
                    S.op("pool", lambda e: e.tensor_tensor(out=nst[0:nk, 0:ncols].bitcast(F32R),
                                                           in0=sst[0:nk, 0:ncols],
                                                           in1=spt[0:nk, 0:ncols], op=ALU.add),
                         reads=[ssR, spR], writes=[nsR])
                ss_prev[T["chain"]] = (nst, nsR)

        def S2(T):
            blk = T["blk"]
            nk, ncols = blk["nk"], T["ncols"]
            spt, spR = T["sp"]
            tt, ttR = R["t"].next()
            S.op("pe", lambda e: e.matmul(tt[0:nk, 0:ncols], lhsT=uincl[0:nk, 0:nk],
                                          rhs=spt[0:nk, 0:ncols], start=True, stop=False),
                 reads=[spR, constR], writes=[ttR])
            if T["ss_in"] is not None:
                sst, ssR = T["ss_in"]
                S.op("pe", lambda e: e.matmul(tt[:, 0:ncols], lhsT=ones_f[:].bitcast(F32R),
                                              rhs=sst[:, 0:ncols].bitcast(F32R), start=False,
                                              stop=False),
                     reads=[ssR, constR], writes=[ttR])
            segs = T["segs"]
            for si, (sg, (kap, vap, kvR)) in enumerate(zip(segs, blk["kv"])):
                S.op("pe", lambda e: e.matmul(tt[0:nk, sg["c0"]:sg["c0"] + sg["n"]], lhsT=kap,
                                              rhs=sg["nqs"], start=False,
                                              stop=(si == len(segs) - 1)),
                     reads=[kvR, sg["qR"]], writes=[ttR])
            T["tt"] = (tt, ttR)

        def S3(T):
            blk = T["blk"]
            nk, mask, ncols = blk["nk"], blk["mask"], T["ncols"]
            tt, ttR = T["tt"]
            wt_, wR_ = R["w"].next()
            S.op("act", lambda e: e.activation(out=wt_[0:nk, 0:ncols], in_=tt[0:nk, 0:ncols],
                                               func=AF.Exp, scale=-1.0),
                 reads=[ttR], writes=[wR_])
            if mask is not None:
                S.op("dve", lambda e: e.tensor_tensor(out=wt_[0:nk, 0:ncols],
                                                      in0=wt_[0:nk, 0:ncols], in1=mask,
                                                      op=ALU.mult),
                     reads=[wR_, constR], writes=[wR_])
            T["w"] = (wt_, wR_)

        def S4(T):
            blk = T["blk"]
            nk = blk["nk"]
            wt_, wR_ = T["w"]
            for sg, (kap, vap, kvR) in zip(T["segs"], blk["kv"]):
                S.op("pe", lambda e: e.matmul(sg["o"], lhsT=vap,
                                              rhs=wt_[0:nk, sg["c0"]:sg["c0"] + sg["n"]],
                                              start=(T["first"] and sg.get("ost", True)),
                                              stop=T["last"],
                                              skip_group_check=(not sg.get("ost", True))),
                     reads=[kvR, wR_], writes=[sg["oR"]])
            if T.get("after") is not None:
                T["after"]()

        stages = (S0, S1, S2, S3, S4)
        n = len(tiles)
        for step in range(n + 4):
            for lag in (4, 3, 2, 1, 0):
                idx = step - lag
                if 0 <= idx < n:
                    stages[lag](tiles[idx])

    def run_interleaved(gens):
        gens = list(gens)
        while gens:
            for g in list(gens):
                try:
                    next(g)
                except StopIteration:
                    gens.remove(g)

    def layer0_sb():
        W0v = w_in0.rearrange("(kc p) n -> p kc n", p=128)
        with S.scope():
            wA = Ring([S.sbuf([128, KC, 4, 128], BF16, "wA%d" % i) for i in range(2)])
            kv_st = Ring([S.sbuf([128, 4, 256], F32, "kv_st%d" % i) for i in range(2)])
            qTs = S.sbuf([128, TOK], BF16, "qTs")
            nqT = S.sbuf([128, TOK], BF16, "nqT")
            kT = S.sbuf([128, TOK], BF16, "kT")
            sgT = S.sbuf([128, TOK], BF16, "sgT")
            v_p = S.sbuf([128, NT, 128], BF16, "v_p")
            qR, kR, gR, vR = Region(), Region(), Region(), Region()
            kst = S.sbuf([128, 16, 128], F32, "kst")
            kstR = Region()
            kTp = S.sbuf([128, 2, 4224], BF16, "kTp")
            kTpR = Region()
            vpast = S.sbuf([128, 2, 33, 128], BF16, "vpast")
            vpR = Region()
            S.op("pool", lambda e: e.memset(kTp[:, :, 4096:4224], 0.0), writes=[kTpR])
            S.op("pool", lambda e: e.memset(vpast[:, :, 32, :], 0.0), writes=[vpR])
            mdiag = S.sbuf([128, 4, 512], BF16, "mdiag")
            msamp = S.sbuf([128, 4, 64], BF16, "msamp")
            for dj in range(4):
                S.op("pool", lambda e: e.affine_select(
                    out=mdiag[:, dj, :], in_=ones_b[:, :], pattern=[[1, 512]],
                    compare_op=ALU.is_gt, fill=0.0, base=-128 * dj, channel_multiplier=-1),
                    reads=[constR], writes=[constR])
            S.op("pool", lambda e: e.affine_select(
                out=msamp[:], in_=ones_b[:, 0:256].rearrange("p (a b) -> p a b", a=4),
                pattern=[[0, 4], [1, 64]], compare_op=ALU.is_gt, fill=0.0, base=0,
                channel_multiplier=-1), reads=[constR], writes=[constR])
            R = {
                "zt": Ring([S.psum([128, 512], F32, "zt%d" % i) for i in range(2)]),
                "t": Ring([S.psum([128, 512], F32, "tt%d" % i) for i in range(2)]),
                "e": Ring([S.sbuf([128, 512], F32, "e%d" % i) for i in range(2)]),
                "sp": Ring([S.sbuf([128, 512], BF16, "sp%d" % i) for i in range(3)]),
                "ss": Ring([S.sbuf([128, 512], F32, "ss%d" % i) for i in range(6)]),
                "w": Ring([S.sbuf([128, 512], BF16, "w%d" % i) for i in range(3)]),
            }
            obank = Ring([S.psum([128, 512], F32, "ob%d" % i) for i in range(2)])
            pproj = Ring([S.psum([128, 512], F32, "pp%d" % i) for i in range(2)])
            ogs = Ring([S.sbuf([128, 512], BF16, "og%d" % i) for i in range(2)])

            def load_wA(p):
                wt, wR = wA.next()
                for s in range(4):
                    c0 = s * 1024 + p * 128
                    S.dma("pool", wt[:, :, s, :], W0v[:, :, c0:c0 + 128], writes=[wR])
                return wt, wR

            def proj(p, wt, wR):
                for s in (0, 1, 3):
                    for (t0, n) in TG:
                        ps, psR = pproj.next()
                        for kc in range(KC):
                            S.op("pe", lambda e: e.matmul(
                                ps[:, 0:n], lhsT=wt[:, kc, s, :], rhs=xT[:, kc, t0:t0 + n],
                                start=(kc == 0), stop=(kc == KC - 1)),
                                reads=[wR] + xT_R[t0 // 128:(t0 + n) // 128], writes=[psR])
                        if s == 0:
                            S.op("dve", lambda e: e.tensor_scalar(
                                out=qTs[:, t0:t0 + n], in0=ps[:, 0:n], scalar1=0.125, scalar2=None,
                                op0=ALU.mult), reads=[psR], writes=[qR])
                            S.op("dve", lambda e: e.tensor_scalar(
                                out=nqT[:, t0:t0 + n], in0=ps[:, 0:n], scalar1=-0.125,
                                scalar2=None, op0=ALU.mult), reads=[psR], writes=[qR])
                        elif s == 1:
                            S.op("dve", lambda e: e.tensor_copy(out=kT[:, t0:t0 + n],
                                                                in_=ps[:, 0:n]),
                                 reads=[psR], writes=[kR])
                        else:
                            S.op("act", lambda e: e.activation(out=sgT[:, t0:t0 + n],
                                                               in_=ps[:, 0:n], func=AF.Silu),
                                 reads=[psR], writes=[gR])
                    yield
                for t0 in range(0, NT, 4):
                    tl = list(range(t0, min(t0 + 4, NT)))
                    stg, stR = kv_st.next()
                    for i, t in enumerate(tl):
                        ps, psR = pproj.next()
                        for kc in range(KC):
                            S.op("pe", lambda e: e.matmul(
                                ps[:, 0:256], lhsT=xT[:, kc, t * 128:(t + 1) * 128],
                                rhs=wt[:, kc, 1:3, :], start=(kc == 0), stop=(kc == KC - 1)),
                                reads=[xT_R[t], wR], writes=[psR])
                        S.op("dve", lambda e: e.tensor_copy(out=stg[:, i, :], in_=ps[:, 0:256]),
                             reads=[psR], writes=[stR])
                        S.op("pool", lambda e: e.tensor_copy(out=v_p[:, t, :],
                                                             in_=stg[:, i, 128:256]),
                             reads=[stR], writes=[vR])
                    pt_tiles = [t for t in tl if t < 16]
                    if pt_tiles:
                        n = len(pt_tiles)
                        r0 = pt_tiles[0] * 128
                        for s_, dst in ((0, o_sbk_p), (1, o_sbv_p)):
                            S.dma("sp",
                                  dst[r0:r0 + n * 128, p * 128:(p + 1) * 128].rearrange(
                                      "(t q) c -> q t c", q=128),
                                  stg[:, 0:n, s_ * 128:(s_ + 1) * 128], reads=[stR])
                    if 16 in tl:
                        i = tl.index(16)
                        for s_, dst in ((0, o_sbk_s), (1, o_sbv_s)):
                            S.dma("sp", dst[:, p * 128:(p + 1) * 128],
                                  stg[:, i, s_ * 128:(s_ + 1) * 128], reads=[stR])
                    yield

            def load_sample_kv(p):
                for b in range(2):
                    S.dma("pool", vpast[:, b, 0:32, :],
                          csv[b].rearrange("(blk q) c -> q blk c", q=128)[:, :, p * 128:(p + 1) * 128],
                          writes=[vpR])
                for b in range(2):
                    for half in range(2):
                        S.dma("sp", kst[:],
                              csk[b, half * 2048:(half + 1) * 2048, :].rearrange(
                                  "(blk q) c -> q blk c", q=128)[:, :, p * 128:(p + 1) * 128],
                              writes=[kstR])
                        for g in range(4):
                            ps, psR = pproj.next()
                            for j in range(4):
                                S.op("pe", lambda e: e.transpose(
                                    ps[:, j * 128:(j + 1) * 128], kst[:, g * 4 + j, :], ident_f[:]),
                                    reads=[kstR, constR], writes=[psR])
                            c0 = half * 2048 + g * 512
                            S.op("dve", lambda e: e.tensor_copy(out=kTp[:, b, c0:c0 + 512],
                                                                in_=ps[:]),
                                 reads=[psR], writes=[kTpR])
                        yield

            def attn(p):
                tiles = []
                for Q in range(4):
                    ob, obR = obank.next()
                    per_head = []
                    for hh in range(2):
                        hs = slice(hh * 64, hh * 64 + 64)
                        seg = dict(c0=0, n=512, hh=hh, qs=qTs[hs, Q * 512:(Q + 1) * 512],
                                   nqs=nqT[hs, Q * 512:(Q + 1) * 512], qR=qR,
                                   o=ob[hs, :], oR=obR)
                        tl = []
                        js = list(range(4 * Q + 3, -1, -1))
                        for bi, j in enumerate(js):
                            dj = j - 4 * Q
                            blk = dict(nk=128, mask=(mdiag[:, dj, :] if dj >= 0 else None),
                                       kv=[(kT[hs, j * 128:(j + 1) * 128], v_p[:, j, hs], kvR_all)])
                            tl.append(dict(segs=[seg], blk=blk, ncols=512, chain=(p, Q, hh),
                                           first=(bi == 0), last=(bi == len(js) - 1), after=None))
                        per_head.append(tl)

                    def fin(ob=ob, obR=obR, Q=Q):
                        og, ogR = ogs.next()
                        S.op("dve", lambda e: e.tensor_tensor(
                            out=og[:], in0=ob[:], in1=sgT[:, Q * 512:(Q + 1) * 512], op=ALU.mult),
                            reads=[obR, gR], writes=[ogR])
                        S.dma("sp", oT0[p, :, Q * 512:(Q + 1) * 512], og[:], reads=[ogR],
                              writes=[oT0_R[p]])
                    per_head[1][-1]["after"] = fin
                    for t0_, t1_ in zip(per_head[0], per_head[1]):
                        tiles.append(t0_)
                        tiles.append(t1_)
                ob, obR = obank.next()
                per_head = []
                for hh in range(2):
                    hs = slice(hh * 64, hh * 64 + 64)
                    segs = []
                    for b in range(2):
                        tq = slice(2048 + 64 * b, 2048 + 64 * b + 64)
                        segs.append(dict(c0=b * 64, n=64, hh=hh, b=b, qs=qTs[hs, tq],
                                         nqs=nqT[hs, tq], qR=qR, ost=(b == 0),
                                         o=ob[hs, b * 64:(b + 1) * 64], oR=obR))
                    tl = []
                    for bi, j in enumerate(range(32, -1, -1)):
                        kv = []
                        for sg in segs:
                            kv.append((kTp[hs, sg["b"], j * 128:j * 128 + 128],
                                       vpast[0:128, sg["b"], j, hs], kvR_all))
                        blk = dict(nk=128, kv=kv, mask=(
                            msamp[:, 0:2, :].rearrange("p a b -> p (a b)") if j == 32 else None))
                        tl.append(dict(segs=segs, blk=blk, ncols=128, chain=(p, 9, hh),
                                       first=(bi == 0), last=(bi == 32), after=None))
                    per_head.append(tl)

                def fin_s(ob=ob, obR=obR):
                    og, ogR = ogs.next()
                    S.op("dve", lambda e: e.tensor_tensor(
                        out=og[:, 0:128], in0=ob[:, 0:128], in1=sgT[:, 2048:2176], op=ALU.mult),
                        reads=[obR, gR], writes=[ogR])
                    S.dma("sp", oT0[p, :, 2048:2176], og[:, 0:128], reads=[ogR], writes=[oT0_R[p]])
                per_head[1][-1]["after"] = fin_s
                for t0_, t1_ in zip(per_head[0], per_head[1]):
                    tiles.append(t0_)
                    tiles.append(t1_)
                sb_pipeline(R, tiles)

            kvR_all = Region()
            dummy = S.sbuf([128, 8], F32, "dmy_sync")

            npairs = NPAIRS_DBG if DEBUG else 8
            nxt = load_wA(0)
            for p in range(npairs):
                wt, wR = nxt
                if p + 1 < npairs:
                    nxt = load_wA(p + 1)
                for _ in proj(p, wt, wR):
                    pass
                for _ in load_sample_kv(p):
                    pass
                for b in range(2):
                    S.op("pool", lambda e: e.tensor_copy(
                        out=kTp[:, b, 4096:4160], in_=kT[:, 2048 + 64 * b:2048 + 64 * b + 64]),
                        reads=[kR], writes=[kTpR])
                    S.dma("sp", vpast[0:64, b, 32, :], v_p[64 * b:64 * b + 64, 16, :],
                          reads=[vR], writes=[vpR])
                S.op("pool", lambda e: e.memset(dummy[:, 0:1], 0.0),
                     reads=[kR, vR, kTpR, vpR], writes=[kvR_all])
                attn(p)
                S.op("pool", lambda e: e.memset(dummy[:, 0:1], 0.0),
                     reads=[kvR_all], writes=[kR, vR, kTpR, vpR, qR, gR])

    zs = dscr("zs", [TOK, 2048], F32)
    zs_R = Region()
    xbcT = dscr("xbcT", [24, 128, TOK], BF16)
    xbcT_R = Region()
    dt_raw = S.sbuf([128, NT, 32], F32, "dt_raw")
    dtR = Region()

    def layer0_ssd_proj():
        W0v = w_in0.rearrange("(kc p) n -> p kc n", p=128)
        with S.scope():
            wB = Ring([S.sbuf([128, KC, 512], BF16, "wB%d" % i) for i in range(2)])
            wD = S.sbuf([128, KC, 32], BF16, "wD")
            wDR = Region()
            pp = Ring([S.psum([128, 512], F32, "sp_pp%d" % i) for i in range(4)])
            zst = Ring([S.sbuf([128, 512], F32, "zst%d" % i) for i in range(3)])
            xst = Ring([S.sbuf([128, TOK], BF16, "xst%d" % i) for i in range(2)])
            cst = Ring([S.sbuf([128, 512], F32, "cst%d" % i) for i in range(2)])
            ev = [0]

            def evac(out, in_, reads, writes, func=None):
                ev[0] += 1
                if func is not None:
                    S.op("act", lambda e: e.activation(out=out, in_=in_, func=func), reads=reads,
                         writes=writes)
                elif ev[0] % 2:
                    S.op("dve", lambda e: e.tensor_copy(out=out, in_=in_), reads=reads,
                         writes=writes)
                else:
                    S.op("act", lambda e: e.copy(out=out, in_=in_), reads=reads, writes=writes)

            def load_wB(c0):
                wt, wR = wB.next()
                S.dma("pool", wt[:], W0v[:, :, c0:c0 + 512], writes=[wR])
                return wt, wR

            S.dma("pool", wD[:], W0v[:, :, 9216:9248], writes=[wDR])
            blocks = [("z", 4096 + i * 512, i) for i in range(4)] + \
                     [("x", 6144 + i * 512, i) for i in range(6)]
            nxt = load_wB(blocks[0][1])
            for bi, (kind, c0, i) in enumerate(blocks):
                wt, wR = nxt
                if bi + 1 < len(blocks):
                    nxt = load_wB(blocks[bi + 1][1])
                if kind == "z":
                    for t in range(NT):
                        ps, psR = pp.next()
                        for kc in range(KC):
                            S.op("pe", lambda e: e.matmul(
                                ps[:], lhsT=xT[:, kc, t * 128:(t + 1) * 128], rhs=wt[:, kc, :],
                                start=(kc == 0), stop=(kc == KC - 1)),
                                reads=[xT_R[t], wR], writes=[psR])
                        zt_, ztR_ = zst.next()
                        evac(zt_[:], ps[:], [psR], [ztR_], func=AF.Silu)
                        S.dma("sp", zs[t * 128:(t + 1) * 128, i * 512:(i + 1) * 512], zt_[:],
                              reads=[ztR_], writes=[zs_R])
                else:
                    for t in (15, 16):
                        ps, psR = pp.next()
                        for kc in range(KC):
                            S.op("pe", lambda e: e.matmul(
                                ps[:], lhsT=xT[:, kc, t * 128:(t + 1) * 128], rhs=wt[:, kc, :],
                                start=(kc == 0), stop=(kc == KC - 1)),
                                reads=[xT_R[t], wR], writes=[psR])
                        ct_, cR_ = cst.next()
                        evac(ct_[:], ps[:], [psR], [cR_])
                        cs = slice(i * 512, (i + 1) * 512)
                        if t == 15:
                            S.dma("sp", o_conv[0:3, cs], ct_[125:128, :], reads=[cR_])
                        else:
                            S.dma("sp", o_conv[3:6, cs], ct_[61:64, :], reads=[cR_])
                            S.dma("sp", o_conv[6:9, cs], ct_[125:128, :], reads=[cR_])
                    for c4 in range(4):
                        cc = i * 4 + c4
                        xt_, xR_ = xst.next()
                        for (t0, n) in TG:
                            ps, psR = pp.next()
                            for kc in range(KC):
                                S.op("pe", lambda e: e.matmul(
                                    ps[:, 0:n], lhsT=wt[:, kc, c4 * 128:(c4 + 1) * 128],
                                    rhs=xT[:, kc, t0:t0 + n], start=(kc == 0), stop=(kc == KC - 1)),
                                    reads=[wR] + xT_R[t0 // 128:(t0 + n) // 128], writes=[psR])
                            evac(xt_[:, t0:t0 + n], ps[:, 0:n], [psR], [xR_])
                        S.dma("sp", xbcT[cc], xt_[:], reads=[xR_], writes=[xbcT_R])
            for t in range(NT):
                ps, psR = pp.next()
                for kc in range(KC):
                    S.op("pe", lambda e: e.matmul(
                        ps[:, 0:32], lhsT=xT[:, kc, t * 128:(t + 1) * 128], rhs=wD[:, kc, :],
                        start=(kc == 0), stop=(kc == KC - 1)),
                        reads=[xT_R[t], wDR], writes=[psR])
                evac(dt_raw[:, t, :], ps[:, 0:32], [psR], [dtR])

    def ssd_phase():
        with S.scope():
            cR = Region()
            cwT = S.sbuf([128, 24, 4], F32, "cwT")
            cbT = S.sbuf([128, 24], F32, "cbT")
            cb_row = S.sbuf([1, 2560], BF16, "cb_row")
            dtb_bc = S.sbuf([128, 32], F32, "dtb_bc")
            a_bc = S.sbuf([128, 32], F32, "a_bc")
            d_bc = S.sbuf([128, 32], F32, "d_bc")
            nw_bc = S.sbuf([128, 2048], F32, "nw_bc")
            with nc.allow_non_contiguous_dma(reason="tiny transposed parameter loads"):
                for j in range(4):
                    S.dma("sp", cwT[:, :, j], conv_w[j].rearrange("(cc p) -> p cc", p=128),
                          writes=[cR])
                S.dma("sp", cbT[:], conv_b.rearrange("(cc p) -> p cc", p=128), writes=[cR])
            for i5 in range(5):
                S.dma("pool", cb_row[:, i5 * 512:(i5 + 1) * 512],
                      conv_b[i5 * 512:(i5 + 1) * 512].rearrange("(o n) -> o n", o=1), writes=[cR])
            S.dma("sp", dtb_bc[:], dt_bias.partition_broadcast(128), writes=[cR])
            S.dma("sp", a_bc[:], a_log.partition_broadcast(128), writes=[cR])
            S.dma("sp", d_bc[:], d_skip.partition_broadcast(128), writes=[cR])
            S.dma("sp", nw_bc[:], norm_w.partition_broadcast(128), writes=[cR])
            S.op("act", lambda e: e.activation(out=a_bc[:], in_=a_bc[:], func=AF.Exp),
                 reads=[cR], writes=[cR])
            S.op("dve", lambda e: e.tensor_scalar(out=a_bc[:], in0=a_bc[:], scalar1=-1.0,
                                                  scalar2=None, op0=ALU.mult),
                 reads=[cR], writes=[cR])
            diagw = S.sbuf([128, 24, 4, 128], BF16, "diagw")
            for cc in range(24):
                for j in range(4):
                    S.op("dve", lambda e: e.tensor_scalar(
                        out=diagw[:, cc, j, :], in0=ident_b[:], scalar1=cwT[:, cc, j:j + 1],
                        scalar2=None, op0=ALU.mult), reads=[cR, constR], writes=[cR])
            Dd = S.sbuf([128, 32, 128], BF16, "Dd")
            for h in range(32):
                S.op("dve", lambda e: e.tensor_scalar(
                    out=Dd[:, h, :], in0=ident_b[:], scalar1=d_bc[:, h:h + 1], scalar2=None,
                    op0=ALU.mult), reads=[cR, constR], writes=[cR])
            ones128 = S.sbuf([128, 128], F32, "ones128")
            Mtri = S.sbuf([128, 128], F32, "Mtri")
            Tcum = S.sbuf([128, 128], F32, "Tcum")
            Bones = S.sbuf([128, 128], F32, "Bones")
            Cones = S.sbuf([128, 2, 128], F32, "Cones")
            LT = S.sbuf([128, 64], F32, "LT")
            BDm = S.sbuf([128, 128], F32, "BDm")
            S.op("pool", lambda e: e.memset(ones128[:], 1.0), writes=[cR])
            S.op("pool", lambda e: e.affine_select(
                out=Mtri[:], in_=ones128[:], pattern=[[-1, 128]], compare_op=ALU.is_gt, fill=0.0,
                base=0, channel_multiplier=1), reads=[cR], writes=[cR])
            S.op("pool", lambda e: e.affine_select(
                out=Tcum[:], in_=ones128[:], pattern=[[1, 128]], compare_op=ALU.is_ge, fill=0.0,
                base=0, channel_multiplier=-1), reads=[cR], writes=[cR])
            S.op("pool", lambda e: e.memset(Bones[:], 1.0), writes=[cR])
            S.op("pool", lambda e: e.memset(Cones[:], 0.0), writes=[cR])
            S.op("pool", lambda e: e.memset(Cones[0:64, 0, :], 1.0), writes=[cR])
            S.op("pool", lambda e: e.memset(Cones[64:128, 1, :], 1.0), writes=[cR])
            for m in (Mtri, Tcum, Bones):
                S.op("pool", lambda e: e.memset(m[0:64, 64:128], 0.0), writes=[cR])
                S.op("pool", lambda e: e.memset(m[64:128, 0:64], 0.0), writes=[cR])
            for hb in (0, 64):
                S.op("pool", lambda e: e.affine_select(
                    out=LT[hb:hb + 64, :], in_=ones128[hb:hb + 64, 0:64], pattern=[[1, 64]],
                    compare_op=ALU.is_ge, fill=0.0, base=0, channel_multiplier=-1),
                    reads=[cR], writes=[cR])
            S.op("pool", lambda e: e.tensor_copy(out=BDm[:], in_=Tcum[:]), reads=[cR], writes=[cR])

            hT = S.sbuf([128, 2048], F32, "hT")
            hTR = Region()
            hS = [S.sbuf([128, 2048], F32, "hS%d" % b) for b in range(2)]
            hSR = [Region(), Region()]
            hb_ring = Ring([S.sbuf([128, 2048], BF16, "hTb%d" % i) for i in range(3)])
            S.op("pool", lambda e: e.memset(hT[:], 0.0), writes=[hTR])

            bank = Ring([S.psum([128, 512], F32, "sbk%d" % i) for i in range(7)])
            tbank = Ring([S.psum([128, 8, 128], BF16, "stb%d" % i) for i in range(1)])
            ldst = S.sbuf([128, 16, 128], F32, "ldst")
            ldR = Region()

            for b in range(2):
                S.dma("sp", ldst[:], sssm[b].rearrange("(cc p) n -> p cc n", p=128), writes=[ldR])
                for g in range(4):
                    ps, psR = bank.next()
                    for j in range(4):
                        S.op("pe", lambda e: e.transpose(ps[:, j * 128:(j + 1) * 128],
                                                         ldst[:, g * 4 + j, :], ident_f[:]),
                             reads=[ldR, constR], writes=[psR])
                    S.op("dve", lambda e: e.tensor_copy(out=hS[b][:, g * 512:(g + 1) * 512],
                                                        in_=ps[:]), reads=[psR], writes=[hSR[b]])

            def store_state(src, srcR, idx):
                for g in range(4):
                    ps, psR = bank.next()
                    for j in range(4):
                        cc = g * 4 + j
                        S.op("pe", lambda e: e.transpose(ps[:, j * 128:(j + 1) * 128],
                                                         src[:, cc * 128:(cc + 1) * 128], ident_f[:]),
                             reads=[srcR, constR], writes=[psR])
                    S.op("dve", lambda e: e.tensor_copy(
                        out=ldst[:, g * 4:(g + 1) * 4, :],
                        in_=ps[:].rearrange("p (a b) -> p a b", a=4)), reads=[psR], writes=[ldR])
                S.dma("sp", o_ssm[idx].rearrange("(cc p) n -> p cc n", p=128), ldst[:], reads=[ldR])

            win = Ring([S.sbuf([128, 24, 134], BF16, "win%d" % i) for i in range(2)])
            zin = Ring([S.sbuf([128, 2048], F32, "zin%d" % i) for i in range(2)])
            xsb = S.sbuf([128, 2560], BF16, "xsb"); xsbR = Region()
            BT = S.sbuf([128, 4, 128], BF16, "BT"); CT = S.sbuf([128, 4, 128], BF16, "CT")
            bcR = Region()
            sm = S.sbuf([128, 8, 32], F32, "sm"); smR = Region()
            Xt = S.sbuf([128, 32, 64], F32, "Xt"); XR = Region()
            Et = S.sbuf([128, 32, 64], F32, "Et"); ER = Region()
            CBm = S.sbuf([128, 4, 128], F32, "CBm"); CBR = Region()
            Wbd = S.sbuf([128, 32, 128], BF16, "Wbd"); WR = Region()
            xdt = S.sbuf([128, 2048], BF16, "xdt"); xdtR = Region()
            xw = S.sbuf([128, 2048], BF16, "xw"); xwR = Region()
            yt = S.sbuf([128, 2048], F32, "yt"); ytR = Region()
            tmp = Ring([S.sbuf([128, 512], F32, "stmp%d" % i) for i in range(2)])
            ssq = S.sbuf([128, 8], F32, "ssq"); ssqR = Region()
            junk = S.sbuf([128, 512], BF16, "junk"); junkR = Region()
            obt = S.sbuf([128, 2048], BF16, "obt"); obtR = Region()
            obT = Ring([S.sbuf([128, 16, 128], BF16, "obT%d" % i) for i in range(2)])
            with nc.allow_non_contiguous_dma(reason="3-row conv state halo"):
                pass

            for t in range(NT):
                samp = (t == 16)
                subs = [(0, 64, 0), (67, 64, 64)] if samp else [(0, 128, 0)]
                wt_, wR_ = win.next()
                srcv = xbcT.rearrange("c p w -> p c w")
                if samp:
                    for b in range(2):
                        S.dma("sp", wt_[:, :, 67 * b + 3:67 * b + 67],
                              srcv[:, :, 2048 + 64 * b:2048 + 64 * b + 64],
                              reads=[xbcT_R], writes=[wR_])
                        with nc.allow_non_contiguous_dma(reason="3-row conv state halo"):
                            for r in range(3):
                                S.dma("pool", wt_[:, :, 67 * b + r],
                                      sconv[b, r].rearrange("(cc p) -> p cc", p=128), writes=[wR_])
                elif t == 0:
                    S.dma("sp", wt_[:, :, 3:131], srcv[:, :, 0:128], reads=[xbcT_R], writes=[wR_])
                    S.op("pool", lambda e: e.memset(wt_[:, :, 0:3], 0.0), writes=[wR_])
                else:
                    S.dma("sp", wt_[:, :, 0:131], srcv[:, :, t * 128 - 3:t * 128 + 128],
                          reads=[xbcT_R], writes=[wR_])
                zt_, zR_ = zin.next()
                S.dma("sp", zt_[:], zs[t * 128:(t + 1) * 128, :], reads=[zs_R], writes=[zR_])

                for i in range(5):
                    ps, psR = bank.next()
                    for (c0, ntok, prow) in subs:
                        S.op("pe", lambda e: e.matmul(
                            ps[prow:prow + ntok, :], lhsT=ones_b[0:1, 0:ntok],
                            rhs=cb_row[0:1, i * 512:(i + 1) * 512], start=True, stop=False),
                            reads=[cR, constR], writes=[psR])
                        for c4 in range(4):
                            cc = i * 4 + c4
                            for j in range(4):
                                S.op("pe", lambda e: e.matmul(
                                    ps[prow:prow + ntok, c4 * 128:(c4 + 1) * 128],
                                    lhsT=wt_[:, cc, c0 + j:c0 + j + ntok], rhs=diagw[:, cc, j, :],
                                    start=False, stop=(c4 == 3 and j == 3)),
                                    reads=[wR_, cR], writes=[psR])
                    S.op("act", lambda e: e.activation(out=xsb[:, i * 512:(i + 1) * 512], in_=ps[:],
                                                       func=AF.Silu), reads=[psR], writes=[xsbR])
                for half, dstT in ((0, BT), (1, CT)):
                    ps, psR = bank.next()
                    for g in range(4):
                        cc = 16 + half * 4 + g
                        for (c0, ntok, prow) in subs:
                            for j in range(4):
                                S.op("pe", lambda e: e.matmul(
                                    ps[:, g * 128 + prow:g * 128 + prow + ntok],
                                    lhsT=diagw[:, cc, j, :], rhs=wt_[:, cc, c0 + j:c0 + j + ntok],
                                    start=(j == 0), stop=(j == 3)),
                                    reads=[wR_, cR], writes=[psR])
                    for g in range(4):
                        cc = 16 + half * 4 + g
                        S.op("act", lambda e: e.activation(
                            out=dstT[:, g, :], in_=ps[:, g * 128:(g + 1) * 128], func=AF.Silu,
                            bias=cbT[:, cc:cc + 1], scale=1.0), reads=[psR, cR], writes=[bcR])
                S.op("dve", lambda e: e.tensor_tensor(out=sm[:, 0, :], in0=dt_raw[:, t, :],
                                                      in1=dtb_bc[:], op=ALU.add),
                     reads=[dtR, cR], writes=[smR])
                S.op("act", lambda e: e.activation(out=sm[:, 0, :], in_=sm[:, 0, :], func=AF.Exp),
                     reads=[smR], writes=[smR])
                S.op("act", lambda e: e.activation(out=sm[:, 1, :], in_=sm[:, 0, :], func=AF.Ln,
                                                   bias=1.0, scale=1.0), reads=[smR], writes=[smR])
                S.op("dve", lambda e: e.tensor_tensor(out=sm[:, 2, :], in0=sm[:, 1, :], in1=a_bc[:],
                                                      op=ALU.mult), reads=[smR, cR], writes=[smR])
                ps, psR = bank.next()
                S.op("pe", lambda e: e.matmul(ps[:, 0:32], lhsT=Tcum[:], rhs=sm[:, 2, :], start=True,
                                              stop=True), reads=[smR, cR], writes=[psR])
                ps2, psR2 = bank.next()
                S.op("pe", lambda e: e.matmul(ps2[:, 0:32], lhsT=Bones[:], rhs=sm[:, 2, :],
                                              start=True, stop=True), reads=[smR, cR], writes=[psR2])
                ps3, psR3 = bank.next()
                for c in range(2):
                    S.op("pe", lambda e: e.matmul(ps3[:, c * 32:(c + 1) * 32], lhsT=Cones[:, c, :],
                                                  rhs=sm[:, 2, :], start=True, stop=True),
                         reads=[smR, cR], writes=[psR3])
                S.op("dve", lambda e: e.tensor_copy(out=sm[:, 3, :], in_=ps[:, 0:32]),
                     reads=[psR], writes=[smR])
                S.op("act", lambda e: e.activation(out=sm[:, 4, :], in_=ps[:, 0:32], func=AF.Exp),
                     reads=[psR], writes=[smR])
                S.op("dve", lambda e: e.tensor_tensor(out=sm[:, 5, :], in0=ps2[:, 0:32],
                                                      in1=sm[:, 3, :], op=ALU.subtract),
                     reads=[psR2, smR], writes=[smR])
                S.op("act", lambda e: e.activation(out=sm[:, 5, :], in_=sm[:, 5, :], func=AF.Exp),
                     reads=[smR], writes=[smR])
                S.op("dve", lambda e: e.tensor_tensor(out=sm[:, 5, :], in0=sm[:, 5, :],
                                                      in1=sm[:, 1, :], op=ALU.mult),
                     reads=[smR], writes=[smR])
                S.op("act", lambda e: e.activation(
                    out=sm[:, 6:8, :], in_=ps3[:, 0:64].rearrange("p (c h) -> p c h", c=2),
                    func=AF.Exp), reads=[psR3], writes=[smR])
                xs3 = xsb[:, 0:2048].rearrange("p (h d) -> p h d", h=32)
                S.op("pool", lambda e: e.tensor_tensor(
                    out=xdt[:].rearrange("p (h d) -> p h d", h=32), in0=xs3,
                    in1=sm[:, 1, :].unsqueeze(2).broadcast_to([128, 32, 64]), op=ALU.mult),
                    reads=[xsbR, smR], writes=[xdtR])
                S.op("pool", lambda e: e.tensor_tensor(
                    out=xw[:].rearrange("p (h d) -> p h d", h=32), in0=xs3,
                    in1=sm[:, 5, :].unsqueeze(2).broadcast_to([128, 32, 64]), op=ALU.mult),
                    reads=[xsbR, smR], writes=[xwR])
                S.op("dve", lambda e: e.tensor_tensor(
                    out=Xt[:], in0=sm[:, 2, :].unsqueeze(2).broadcast_to([128, 32, 64]),
                    in1=LT[:].unsqueeze(1).broadcast_to([128, 32, 64]), op=ALU.mult),
                    reads=[smR, cR], writes=[XR])
                for q4 in range(4):
                    ps, psR = bank.next()
                    S.op("pe", lambda e: e.matmul(
                        ps[:], lhsT=Mtri[:], rhs=Xt[:, q4 * 8:(q4 + 1) * 8, :], start=True, stop=True),
                        reads=[XR, cR], writes=[psR])
                    S.op("act", lambda e: e.activation(
                        out=Et[:, q4 * 8:(q4 + 1) * 8, :],
                        in_=ps[:].rearrange("p (h d) -> p h d", h=8), func=AF.Exp),
                        reads=[psR], writes=[ER])
                ps, psR = bank.next()
                for g in range(4):
                    S.op("pe", lambda e: e.matmul(ps[:, g * 128:(g + 1) * 128], lhsT=BT[:, g, :],
                                                  rhs=CT[:, g, :], start=True, stop=True),
                         reads=[bcR], writes=[psR])
                S.op("dve", lambda e: e.tensor_tensor(
                    out=CBm[:], in0=ps[:].rearrange("p (g t) -> p g t", g=4),
                    in1=BDm[:].unsqueeze(1).broadcast_to([128, 4, 128]), op=ALU.mult),
                    reads=[psR, cR], writes=[CBR])
                for g in range(4):
                    S.op("dve", lambda e: e.tensor_tensor(
                        out=Wbd[:, g * 8:(g + 1) * 8, :].rearrange("p h (c t) -> p h c t", c=2),
                        in0=Et[:, g * 8:(g + 1) * 8, :].unsqueeze(2).broadcast_to([128, 8, 2, 64]),
                        in1=CBm[:, g, :].rearrange("p (c t) -> p c t", c=2).unsqueeze(1).broadcast_to(
                            [128, 8, 2, 64]), op=ALU.mult),
                        reads=[ER, CBR], writes=[WR])
                hin = []
                for c in range(2):
                    if samp:
                        hin.append((hS[c], hSR[c]))
                    else:
                        hin.append((hT, hTR))
                hb0, hb0R = hb_ring.next()
                S.op("act", lambda e: e.copy(out=hb0[:], in_=hin[0][0][:]), reads=[hin[0][1]],
                     writes=[hb0R])
                hbs = [(hb0, hb0R), None]
                ypsl = []
                for g in range(4):
                    ps, psR = bank.next()
                    for hh in range(8):
                        h = g * 8 + hh
                        S.op("pe", lambda e: e.matmul(ps[:, hh * 64:(hh + 1) * 64], lhsT=Wbd[:, h, :],
                                                      rhs=xdt[:, h * 64:(h + 1) * 64], start=True,
                                                      stop=False),
                             reads=[WR, xdtR], writes=[psR])
                        S.op("pe", lambda e: e.matmul(ps[:, hh * 64:(hh + 1) * 64], lhsT=Dd[:, h, :],
                                                      rhs=xsb[:, h * 64:(h + 1) * 64], start=False,
                                                      stop=True),
                             reads=[cR, xsbR], writes=[psR])
                    S.op("dve", lambda e: e.tensor_copy(out=yt[:, g * 512:(g + 1) * 512], in_=ps[:]),
                         reads=[psR], writes=[ytR])
                for c in range(2):
                    cs = slice(c * 64, (c + 1) * 64)
                    hsrc, hsrcR = hin[c]
                    if c == 1:
                        if samp:
                            hb1, hb1R = hb_ring.next()
                            S.op("act", lambda e: e.copy(out=hb1[:], in_=hsrc[:]), reads=[hsrcR],
                                 writes=[hb1R])
                        else:
                            hb1, hb1R = hb_ring.next()
                            S.op("act", lambda e: e.copy(out=hb1[:], in_=hT[:]), reads=[hTR],
                                 writes=[hb1R])
                        hbs[1] = (hb1, hb1R)
                    hb, hbR = hbs[c]
                    for g in range(4):
                        gs = slice(g * 512, (g + 1) * 512)
                        ps, psR = bank.next()
                        S.op("pe", lambda e: e.matmul(ps[cs, :], lhsT=CT[:, g, cs], rhs=hb[:, gs],
                                                      start=True, stop=True),
                             reads=[bcR, hbR], writes=[psR])
                        tm, tmR = tmp.next()
                        S.op("dve", lambda e: e.tensor_tensor(
                            out=tm[cs, :].rearrange("p (h d) -> p h d", h=8),
                            in0=ps[cs, :].rearrange("p (h d) -> p h d", h=8),
                            in1=sm[cs, 4, g * 8:(g + 1) * 8].unsqueeze(2).broadcast_to([64, 8, 64]),
                            op=ALU.mult), reads=[psR, smR], writes=[tmR])
                        S.op("pool", lambda e: e.tensor_tensor(out=yt[cs, gs], in0=yt[cs, gs],
                                                               in1=tm[cs, :], op=ALU.add),
                             reads=[tmR, ytR], writes=[ytR])
                        ps, psR = bank.next()
                        S.op("pe", lambda e: e.matmul(
                            ps[:], lhsT=xsb[cs, 2048 + g * 128:2048 + (g + 1) * 128], rhs=xw[cs, gs],
                            start=True, stop=True), reads=[xsbR, xwR], writes=[psR])
                        tm, tmR = tmp.next()
                        S.op("pool", lambda e: e.tensor_tensor(
                            out=tm[:].rearrange("p (h d) -> p h d", h=8),
                            in0=hsrc[:, gs].rearrange("p (h d) -> p h d", h=8),
                            in1=sm[:, 6 + c, g * 8:(g + 1) * 8].unsqueeze(2).broadcast_to(
                                [128, 8, 64]), op=ALU.mult),
                            reads=[hsrcR, smR, hbR], writes=[tmR])
                        S.op("dve", lambda e: e.tensor_tensor(out=hsrc[:, gs], in0=tm[:], in1=ps[:],
                                                              op=ALU.add),
                             reads=[tmR, psR, hbR], writes=[hsrcR])
                S.op("pool", lambda e: e.tensor_tensor(out=yt[:], in0=yt[:], in1=zt_[:], op=ALU.mult),
                     reads=[ytR, zR_], writes=[ytR])
                for g in range(4):
                    S.op("act", lambda e: e.activation(
                        out=junk[:], in_=yt[:, g * 512:(g + 1) * 512], func=AF.Square,
                        accum_out=ssq[:, g:g + 1]), reads=[ytR], writes=[junkR, ssqR])
                S.op("dve", lambda e: e.tensor_scalar(
                    out=ssq[:, 4:8], in0=ssq[:, 0:4], scalar1=1.0 / 512.0, scalar2=1e-5,
                    op0=ALU.mult, op1=ALU.add), reads=[ssqR], writes=[ssqR])
                S.op("act", lambda e: e.activation(out=ssq[:, 4:8], in_=ssq[:, 4:8], func=AF.Ln),
                     reads=[ssqR], writes=[ssqR])
                S.op("act", lambda e: e.activation(out=ssq[:, 4:8], in_=ssq[:, 4:8], func=AF.Exp,
                                                   scale=-0.5), reads=[ssqR], writes=[ssqR])
                for g in range(4):
                    gs = slice(g * 512, (g + 1) * 512)
                    S.op("dve", lambda e: e.scalar_tensor_tensor(
                        out=obt[:, gs], in0=yt[:, gs], scalar=ssq[:, 4 + g:5 + g], in1=nw_bc[:, gs],
                        op0=ALU.mult, op1=ALU.mult), reads=[ytR, ssqR, cR], writes=[obtR])
                oT_, oTR_ = obT.next()
                for g2 in range(2):
                    pt, ptR = tbank.next()
                    for j in range(8):
                        cc = g2 * 8 + j
                        S.op("pe", lambda e: e.transpose(pt[:, j, :], obt[:, cc * 128:(cc + 1) * 128],
                                                         ident_b[:]), reads=[obtR, constR], writes=[ptR])
                    S.op("act", lambda e: e.copy(out=oT_[:, g2 * 8:(g2 + 1) * 8, :], in_=pt[:]),
                         reads=[ptR], writes=[oTR_])
                S.dma("sp", oT0[8:24, :, t * 128:(t + 1) * 128].rearrange("c p w -> p c w"), oT_[:],
                      reads=[oTR_], writes=oT0_R[8:24])
                if t == 15:
                    store_state(hT, hTR, 0)
            for b in range(2):
                store_state(hS[b], hSR[b], 1 + b)


    ALPHA = (2 * 2) ** 0.25

    def outproj_ln(oT, oT_R, ncch, w_out, ln_g, ln_b, src_p, src_s, srcR, dst_p, dst_s, dstR):
        with S.scope():
            cR = Region()
            wo = S.sbuf([128, ncch, 2048], BF16, "wo")
            wov = w_out.rearrange("(c p) n -> p c n", p=128)
            for c0 in range(0, ncch, 4):
                S.dma("pool", wo[:, c0:c0 + 4, :], wov[:, c0:c0 + 4, :], writes=[cR])
            g_bc = S.sbuf([128, 2048], F32, "g_bc")
            b_bc = S.sbuf([128, 2048], F32, "b_bc")
            S.dma("sp", g_bc[:], ln_g.partition_broadcast(128), writes=[cR])
            S.dma("sp", b_bc[:], ln_b.partition_broadcast(128), writes=[cR])
            oTt = Ring([S.sbuf([128, ncch, 128], BF16, "oTt%d" % i) for i in range(2)])
            xr = Ring([S.sbuf([128, 2048], F32, "xr%d" % i) for i in range(2)])
            pre = Ring([S.sbuf([128, 2048], F32, "pre%d" % i) for i in range(2)])
            yo = Ring([S.sbuf([128, 2048], F32, "yo%d" % i) for i in range(2)])
            st = Ring([S.sbuf([128, 16], F32, "lnst%d" % i) for i in range(2)])
            junk = S.sbuf([128, 2048], BF16, "lnjunk")
            junkR = Region()
            bank = Ring([S.psum([128, 512], F32, "opb%d" % i) for i in range(8)])
            oTv = oT.rearrange("c p w -> p c w")
            for t in range(NT):
                ot, otR = oTt.next()
                S.dma("sp", ot[:], oTv[:, :, t * 128:(t + 1) * 128], reads=oT_R, writes=[otR])
                xt_, xR_ = xr.next()
                S.dma("sp", xt_[:], x_rows(src_p, src_s, t), reads=[srcR], writes=[xR_])
                pt_, pR_ = pre.next()
                st_, sR_ = st.next()
                for nb in range(4):
                    ps, psR = bank.next()
                    for c in range(ncch):
                        S.op("pe", lambda e: e.matmul(ps[:], lhsT=ot[:, c, :],
                                                      rhs=wo[:, c, nb * 512:(nb + 1) * 512],
                                                      start=(c == 0), stop=(c == ncch - 1)),
                             reads=[otR, cR], writes=[psR])
                    cs = slice(nb * 512, (nb + 1) * 512)
                    S.op("dve", lambda e: e.scalar_tensor_tensor(
                        out=pt_[:, cs], in0=xt_[:, cs], scalar=ALPHA, in1=ps[:], op0=ALU.mult,
                        op1=ALU.add, accum_out=st_[:, nb:nb + 1]),
                        reads=[xR_, psR], writes=[pR_, sR_])
                S.op("act", lambda e: e.activation(out=junk[:], in_=pt_[:], func=AF.Square,
                                                   accum_out=st_[:, 4:5]),
                     reads=[pR_], writes=[junkR, sR_])
                S.op("dve", lambda e: e.tensor_reduce(out=st_[:, 5:6], in_=st_[:, 0:4], axis=AX.X,
                                                      op=ALU.add), reads=[sR_], writes=[sR_])
                S.op("dve", lambda e: e.tensor_scalar(out=st_[:, 6:7], in0=st_[:, 5:6],
                                                      scalar1=1.0 / 2048.0, scalar2=None,
                                                      op0=ALU.mult), reads=[sR_], writes=[sR_])
                S.op("dve", lambda e: e.tensor_tensor(out=st_[:, 7:8], in0=st_[:, 6:7],
                                                      in1=st_[:, 6:7], op=ALU.mult),
                     reads=[sR_], writes=[sR_])
                S.op("dve", lambda e: e.scalar_tensor_tensor(
                    out=st_[:, 8:9], in0=st_[:, 4:5], scalar=1.0 / 2048.0, in1=st_[:, 7:8],
                    op0=ALU.mult, op1=ALU.subtract), reads=[sR_], writes=[sR_])
                S.op("dve", lambda e: e.tensor_scalar(out=st_[:, 8:9], in0=st_[:, 8:9], scalar1=1e-5,
                                                      scalar2=None, op0=ALU.add),
                     reads=[sR_], writes=[sR_])
                S.op("act", lambda e: e.activation(out=st_[:, 9:10], in_=st_[:, 8:9], func=AF.Ln),
                     reads=[sR_], writes=[sR_])
                S.op("act", lambda e: e.activation(out=st_[:, 9:10], in_=st_[:, 9:10], func=AF.Exp,
                                                   scale=-0.5), reads=[sR_], writes=[sR_])
                S.op("dve", lambda e: e.scalar_tensor_tensor(
                    out=st_[:, 10:11], in0=st_[:, 6:7], scalar=-1.0, in1=st_[:, 9:10],
                    op0=ALU.mult, op1=ALU.mult), reads=[sR_], writes=[sR_])
                yt_, yR_ = yo.next()
                S.op("act", lambda e: e.activation(out=yt_[:], in_=pt_[:], func=AF.Identity,
                                                   scale=st_[:, 9:10], bias=st_[:, 10:11]),
                     reads=[pR_, sR_], writes=[yR_])
                S.op("pool", lambda e: e.tensor_tensor(out=yt_[:], in0=yt_[:], in1=g_bc[:],
                                                       op=ALU.mult), reads=[yR_, cR], writes=[yR_])
                S.op("dve", lambda e: e.tensor_tensor(out=yt_[:], in0=yt_[:], in1=b_bc[:],
                                                      op=ALU.add), reads=[yR_, cR], writes=[yR_])
                S.dma("sp", x_rows(dst_p, dst_s, t), yt_[:], reads=[yR_], writes=[dstR])

    oT1 = dscr("oT1", [16, 128, TOK], BF16)
    oT1_R = [Region() for _ in range(16)]
    y0 = dscr("y0", [TOK, 2048], F32)
    y0R = Region()
    Gd = dscr("Gd", [32, 768], F32)
    GdR = Region()

    def layer1_attn():
        W1v = w_in1.rearrange("(kc p) n -> p kc n", p=128)
        with S.scope():
            cR = Region()
            wA = Ring([S.sbuf([128, KC, 4, 128], BF16, "w1A%d" % i) for i in range(2)])
            kv_st = Ring([S.sbuf([128, 4, 256], F32, "kv1st%d" % i) for i in range(2)])
            qTs = S.sbuf([128, TOK], BF16, "q1Ts")
            kT = S.sbuf([128, TOK], BF16, "k1T")
            sgT = S.sbuf([128, TOK], BF16, "sg1T")
            v_p = S.sbuf([128, NT, 128], BF16, "v1_p")
            qR, kR, gR, vR = Region(), Region(), Region(), Region()
            kst = S.sbuf([128, 4, 128], F32, "k1st")
            kstR = Region()
            kTp = S.sbuf([128, 2, 640], BF16, "k1Tp")
            kTpR = Region()
            vpast = S.sbuf([128, 2, 5, 128], BF16, "v1past")
            vpR = Region()
            S.op("pool", lambda e: e.memset(kTp[:, :, 512:640], 0.0), writes=[kTpR])
            S.op("pool", lambda e: e.memset(vpast[:, :, 4, :], 0.0), writes=[vpR])
            rbt = S.sbuf([32, 257], F32, "rbt")
            Gs = S.sbuf([32, 768], F32, "Gs")
            S.dma("sp", rbt[:], rel_bias[:, :], writes=[cR])
            S.op("dve", lambda e: e.tensor_copy(out=Gs[:, 0:256], in_=rbt[:, 1:257]),
                 reads=[cR], writes=[cR])
            S.op("dve", lambda e: e.tensor_copy(out=Gs[:, 256:768],
                                                in_=rbt[:, 256:257].broadcast_to([32, 512])),
                 reads=[cR], writes=[cR])
            S.dma("sp", Gd[:, :], Gs[:], reads=[cR], writes=[GdR])
            Jm = S.sbuf([128, 128], F32, "Jm")
            S.op("pool", lambda e: e.memset(Jm[:], 0.0), writes=[cR])
            S.op("pool", lambda e: e.affine_select(
                out=Jm[:], in_=Jm[:], pattern=[[1, 128]], compare_op=ALU.not_equal, fill=1.0,
                base=-127, channel_multiplier=1), reads=[cR], writes=[cR])
            T1 = S.sbuf([128, 640], F32, "T1")
            T1R = Region()
            eBT = S.sbuf([128, 2, 640], F32, "eBT")
            eBR = Region()
            st_ring = Ring([S.psum([128, 512], F32, "b_st%d" % i) for i in range(2)])
            od_ring = Ring([S.psum([128, 512], F32, "b_od%d" % i) for i in range(4)])
            pproj = Ring([S.psum([128, 512], F32, "b_pp%d" % i) for i in range(2)])
            wraw = Ring([S.sbuf([128, 512], F32, "wraw%d" % i) for i in range(3)])
            wbf = Ring([S.sbuf([128, 512], BF16, "wbf%d" % i) for i in range(4)])
            rec = Ring([S.sbuf([128, 512], F32, "rec%d" % i) for i in range(2)])
            ogs = Ring([S.sbuf([128, 512], BF16, "og1%d" % i) for i in range(2)])
            kvR_all = Region()
            dmy = S.sbuf([128, 8], F32, "dmy1_sync")

            def load_wA(p):
                wt, wR = wA.next()
                for s in range(4):
                    c0 = s * 2048 + p * 128
                    S.dma("pool", wt[:, :, s, :], W1v[:, :, c0:c0 + 128], writes=[wR])
                return wt, wR

            def proj(p, wt, wR):
                for s in (0, 1, 3):
                    for (t0, n) in TG:
                        ps, psR = pproj.next()
                        for kc in range(KC):
                            S.op("pe", lambda e: e.matmul(
                                ps[:, 0:n], lhsT=wt[:, kc, s, :], rhs=xT[:, kc, t0:t0 + n],
                                start=(kc == 0), stop=(kc == KC - 1)),
                                reads=[wR] + xT_R[t0 // 128:(t0 + n) // 128], writes=[psR])
                        if s == 0:
                            S.op("dve", lambda e: e.tensor_scalar(
                                out=qTs[:, t0:t0 + n], in0=ps[:, 0:n], scalar1=0.125, scalar2=None,
                                op0=ALU.mult), reads=[psR], writes=[qR])
                        elif s == 1:
                            S.op("dve", lambda e: e.tensor_copy(out=kT[:, t0:t0 + n],
                                                                in_=ps[:, 0:n]),
                                 reads=[psR], writes=[kR])
                        else:
                            S.op("act", lambda e: e.activation(out=sgT[:, t0:t0 + n],
                                                               in_=ps[:, 0:n], func=AF.Silu),
                                 reads=[psR], writes=[gR])
                for t0 in range(0, NT, 4):
                    tl = list(range(t0, min(t0 + 4, NT)))
                    stg, stR = kv_st.next()
                    for i, t in enumerate(tl):
                        ps, psR = pproj.next()
                        for kc in range(KC):
                            S.op("pe", lambda e: e.matmul(
                                ps[:, 0:256], lhsT=xT[:, kc, t * 128:(t + 1) * 128],
                                rhs=wt[:, kc, 1:3, :], start=(kc == 0), stop=(kc == KC - 1)),
                                reads=[xT_R[t], wR], writes=[psR])
                        S.op("dve", lambda e: e.tensor_copy(out=stg[:, i, :], in_=ps[:, 0:256]),
                             reads=[psR], writes=[stR])
                        S.op("pool", lambda e: e.tensor_copy(out=v_p[:, t, :],
                                                             in_=stg[:, i, 128:256]),
                             reads=[stR], writes=[vR])
                    cs = slice(p * 128, (p + 1) * 128)
                    if t0 == 12:
                        for s_, dst in ((0, o_bk_p), (1, o_bv_p)):
                            S.dma("sp", dst[:, cs].rearrange("(t q) c -> q t c", q=128),
                                  stg[:, 0:4, s_ * 128:(s_ + 1) * 128], reads=[stR])
                    if 16 in tl:
                        i = tl.index(16)
                        for b in range(2):
                            for s_, dst in ((0, o_bk_s), (1, o_bv_s)):
                                S.dma("sp", dst[b, 448:512, cs],
                                      stg[64 * b:64 * b + 64, i, s_ * 128:(s_ + 1) * 128],
                                      reads=[stR])

            def load_sample_kv(p):
                cs = slice(p * 128, (p + 1) * 128)
                for b in range(2):
                    S.dma("pool", vpast[:, b, 0:4, :],
                          cbv[b].rearrange("(blk q) c -> q blk c", q=128)[:, :, cs], writes=[vpR])
                    S.dma("sp", kst[:], cbk[b].rearrange("(blk q) c -> q blk c", q=128)[:, :, cs],
                          writes=[kstR])
                    ps, psR = pproj.next()
                    for j in range(4):
                        S.op("pe", lambda e: e.transpose(ps[:, j * 128:(j + 1) * 128], kst[:, j, :],
                                                         ident_f[:]),
                             reads=[kstR, constR], writes=[psR])
                    S.op("dve", lambda e: e.tensor_copy(out=kTp[:, b, 0:512], in_=ps[:]),
                         reads=[psR], writes=[kTpR])
                for b in range(2):
                    S.op("pool", lambda e: e.tensor_copy(
                        out=kTp[:, b, 512:576], in_=kT[:, 2048 + 64 * b:2048 + 64 * b + 64]),
                        reads=[kR], writes=[kTpR])
                    S.dma("sp", vpast[0:64, b, 4, :], v_p[64 * b:64 * b + 64, 16, :],
                          reads=[vR], writes=[vpR])

            def bias_tiles(p):
                for hh in range(2):
                    h = 2 * p + hh
                    src = bass.AP(tensor=Gd.tensor, offset=Gd[h, 0:1].offset, ap=[[1, 128], [1, 640]])
                    S.dma("sp", T1[:], src, reads=[GdR], writes=[T1R])
                    for (c0, n) in ((0, 512), (512, 128)):
                        ps, psR = pproj.next()
                        S.op("pe", lambda e: e.matmul(ps[:, 0:n], lhsT=Jm[:], rhs=T1[:, c0:c0 + n],
                                                      start=True, stop=True),
                             reads=[T1R, cR], writes=[psR])
                        S.op("act", lambda e: e.activation(out=eBT[:, hh, c0:c0 + n], in_=ps[:, 0:n],
                                                           func=AF.Exp), reads=[psR], writes=[eBR])

            def band_block(hs, hh, kap, vap, qap, ncol, bcol0, zero_lo, zero_hi, oacc, dacc, odR,
                           ddR, ocols, first, last):
                box = {}

                def s0():
                    sps, spR = st_ring.next()
                    S.op("pe", lambda e: e.matmul(sps[:, 0:ncol], lhsT=kap, rhs=qap, start=True,
                                                  stop=True), reads=[kvR_all, qR], writes=[spR])
                    box["sps"] = (sps, spR)

                def s1():
                    sps, spR = box["sps"]
                    wr, wrR = wraw.next()
                    S.op("act", lambda e: e.activation(out=wr[:, 0:ncol], in_=sps[:, 0:ncol],
                                                       func=AF.Exp), reads=[spR], writes=[wrR])
                    box["wr"] = (wr, wrR)

                def s2():
                    wr, wrR = box["wr"]
                    wb, wbR = wbf.next()
                    S.op("dve", lambda e: e.tensor_tensor(out=wb[:, 0:ncol], in0=wr[:, 0:ncol],
                                                          in1=eBT[:, hh, bcol0:bcol0 + ncol],
                                                          op=ALU.mult),
                         reads=[wrR, eBR], writes=[wbR])
                    if zero_lo is not None:
                        S.op("pool", lambda e: e.memset(wb[64:128, zero_lo:zero_lo + 64], 0.0),
                             reads=[wbR], writes=[wbR])
                    if zero_hi is not None:
                        S.op("pool", lambda e: e.memset(wb[0:64, zero_hi:zero_hi + 64], 0.0),
                             reads=[wbR], writes=[wbR])
                    box["wb"] = (wb, wbR)

                def s3():
                    wb, wbR = box["wb"]
                    S.op("pe", lambda e: e.matmul(oacc[hs, ocols], lhsT=vap, rhs=wb[:, 0:ncol],
                                                  start=first, stop=last, skip_group_check=True),
                         reads=[kvR_all, wbR], writes=[odR])
                    S.op("pe", lambda e: e.matmul(dacc[hs, ocols], lhsT=ones_b[:, 0:64],
                                                  rhs=wb[:, 0:ncol], start=first, stop=last,
                                                  skip_group_check=True),
                         reads=[constR, wbR], writes=[ddR])
                return (s0, s1, s2, s3)

            def run_pipe(stages, depth=2):
                n = len(stages)
                for step in range(n + 3):
                    for lag in (3, 2, 1, 0):
                        idx = step - lag
                        if 0 <= idx < n:
                            stages[idx][lag]()

            def finish_group(p, oacc, dacc, odR, ddR, ncols, tok0):
                rc, rcR = rec.next()
                S.op("dve", lambda e: e.reciprocal(out=rc[:, 0:ncols], in_=dacc[:, 0:ncols]),
                     reads=[ddR], writes=[rcR])
                S.op("dve", lambda e: e.tensor_tensor(out=rc[:, 0:ncols], in0=oacc[:, 0:ncols],
                                                      in1=rc[:, 0:ncols], op=ALU.mult),
                     reads=[odR, rcR], writes=[rcR])
                og, ogR = ogs.next()
                S.op("pool", lambda e: e.tensor_tensor(out=og[:, 0:ncols], in0=rc[:, 0:ncols],
                                                       in1=sgT[:, tok0:tok0 + ncols], op=ALU.mult),
                     reads=[rcR, gR], writes=[ogR])
                S.dma("sp", oT1[p, :, tok0:tok0 + ncols], og[:, 0:ncols], reads=[ogR],
                      writes=[oT1_R[p]])

            def attn(p):
                for G in range(4):
                    oacc, oR_ = od_ring.next()
                    dacc, dR_ = od_ring.next()
                    odR = oR_
                    stages = []
                    for hh in range(2):
                        hs = slice(hh * 64, hh * 64 + 64)
                        ms = list(range(max(0, 4 * G - 4), 4 * G + 4))
                        for mi, m in enumerate(ms):
                            cmin = max(8 * G, 2 * m)
                            cmax = min(8 * G + 7, 2 * m + 9)
                            ncol = (cmax - cmin + 1) * 64
                            oc0 = (cmin - 8 * G) * 64
                            zero_lo = 0 if cmin == 2 * m else None
                            zero_hi = (ncol - 64) if cmax == 2 * m + 9 else None
                            stages.append(band_block(
                                hs, hh, kT[hs, m * 128:(m + 1) * 128], v_p[:, m, hs],
                                qTs[hs, cmin * 64:cmin * 64 + ncol], ncol,
                                64 * (cmin - 2 * m), zero_lo, zero_hi, oacc, dacc, odR,
                                dR_, slice(oc0, oc0 + ncol), mi == 0, mi == len(ms) - 1))
                    run_pipe(stages)
                    finish_group(p, oacc, dacc, odR, dR_, 512, G * 512)
                oacc, oR_ = od_ring.next()
                dacc, dR_ = od_ring.next()
                odR = oR_
                stages = []
                for hh in range(2):
                    hs = slice(hh * 64, hh * 64 + 64)
                    for b in range(2):
                        tq = slice(2048 + 64 * b, 2048 + 64 * b + 64)
                        for j in range(5):
                            bcol0 = 0 if j == 4 else 512 - 128 * j
                            stages.append(band_block(
                                hs, hh, kTp[hs, b, j * 128:(j + 1) * 128], vpast[:, b, j, hs],
                                qTs[hs, tq], 64, bcol0, (0 if j == 4 else None), None, oacc,
                                dacc, odR, dR_, slice(b * 64, b * 64 + 64),
                                (j == 0 and b == 0), j == 4))
                run_pipe(stages)
                finish_group(p, oacc, dacc, odR, dR_, 128, 2048)

            for b in range(2):
                S.dma("sp", o_bk_s[b, 0:448, :], cbk[b, 64:512, :])
                S.dma("sp", o_bv_s[b, 0:448, :], cbv[b, 64:512, :])

            npairs = NPAIRS_DBG if DEBUG else 16
            nxt = load_wA(0)
            for p in range(npairs):
                wt, wR = nxt
                if p + 1 < npairs:
                    nxt = load_wA(p + 1)
                proj(p, wt, wR)
                load_sample_kv(p)
                bias_tiles(p)
                S.op("pool", lambda e: e.memset(dmy[:, 0:1], 0.0),
                     reads=[kR, vR, kTpR, vpR], writes=[kvR_all])
                attn(p)
                S.op("pool", lambda e: e.memset(dmy[:, 0:1], 0.0),
                     reads=[kvR_all], writes=[kR, vR, kTpR, vpR, qR, gR])


    with S.scope():
        xT.t = S.sbuf([128, KC, TOK], BF16, "xT_l0")
        build_xT(xp, xs)
        if not SKIP_SB:
            layer0_sb()
        layer0_ssd_proj()
    ssd_phase()
    if not SKIP_REST:
        outproj_ln(oT0, oT0_R, 24, w_out0, ln_g0, ln_b0, xp, xs, Region(), y0[0:2048, :],
                   y0[2048:2176, :], y0R)
        with S.scope():
            xT.t = S.sbuf([128, KC, TOK], BF16, "xT_l1")
            for r_ in xT_R:
                r_.w = None
                r_.r = {}
            build_xT(y0[0:2048, :], y0[2048:2176, :], srcR=y0R)
            layer1_attn()
        outR = Region()
        outproj_ln(oT1, oT1_R, 16, w_out1, ln_g1, ln_b1, y0[0:2048, :], y0[2048:2176, :], y0R,
                   o_yp, o_ys, outR)

    S.barrier(engines=("sp",))
    print("program: %d instructions, %d waits" % (S.ninst, S.nwait))
    return nc, st


NPAIRS_DBG = 1


def kernel(x_prompt, x_sample, cache_sb_k, cache_sb_v, state_ssm, state_conv, cache_band_k,
           cache_band_v, even_w_in, even_conv_w, even_conv_b, even_dt_bias, even_a_log,
           even_d_skip, even_norm_w, even_w_out, even_ln_g, even_ln_b, odd_w_in, odd_rel_bias,
           odd_w_out, odd_ln_g, odd_ln_b, _debug_cores=None):
    f = lambda a: np.ascontiguousarray(np.asarray(a, dtype=np.float32))
    x_prompt = f(x_prompt)
    x_sample = f(x_sample)
    cache_sb_k = np.asarray(cache_sb_k)
    cache_sb_v = np.asarray(cache_sb_v)
    nc, st = build_program()
    in_maps = []
    for c in range(NCORES):
        in_maps.append({
            "xp": x_prompt[c],
            "xs": x_sample[2 * c:2 * c + 2].reshape(128, D),
            "csk": f(cache_sb_k[0, 2 * c:2 * c + 2]).reshape(2, 4096, 1024),
            "csv": f(cache_sb_v[0, 2 * c:2 * c + 2]).reshape(2, 4096, 1024),
            "w_in0": f(even_w_in[0]),
            "sssm": f(np.asarray(state_ssm)[0, 2 * c:2 * c + 2]).reshape(2, 2048, 128),
            "sconv": f(np.asarray(state_conv)[0, 2 * c:2 * c + 2]),
            "conv_w": f(even_conv_w[0]), "conv_b": f(even_conv_b[0]),
            "dt_bias": f(even_dt_bias[0]), "a_log": f(even_a_log[0]),
            "d_skip": f(even_d_skip[0]), "norm_w": f(even_norm_w[0]),
            "w_out0": f(even_w_out[0]), "ln_g0": f(even_ln_g[0]), "ln_b0": f(even_ln_b[0]),
            "cbk": f(np.asarray(cache_band_k)[0, 2 * c:2 * c + 2]).reshape(2, 512, 2048),
            "cbv": f(np.asarray(cache_band_v)[0, 2 * c:2 * c + 2]).reshape(2, 512, 2048),
            "w_in1": f(odd_w_in[0]), "rel_bias": f(odd_rel_bias[0]), "w_out1": f(odd_w_out[0]),
            "ln_g1": f(odd_ln_g[0]), "ln_b1": f(odd_ln_b[0]),
        })
    res = run_bass_kernel_spmd(nc, in_maps, core_ids=list(range(NCORES)))
    st.close()
    R = res.results
    if DEBUG:
        return R
    cat = lambda name: np.stack([np.asarray(R[c][name]) for c in range(NCORES)])
    B = 8
    y_p = cat("o_yp")
    y_s = cat("o_ys").reshape(16, 64, D)
    sbk_p = cat("o_sbk_p").reshape(1, B, SEQ, 16, 64)
    sbv_p = cat("o_sbv_p").reshape(1, B, SEQ, 16, 64)
    sbk_s = cat("o_sbk_s").reshape(1, 16, 64, 16, 64)
    sbv_s = cat("o_sbv_s").reshape(1, 16, 64, 16, 64)
    ssm = cat("o_ssm")
    ssm_p = ssm[:, 0].reshape(1, B, 32, 64, 128)
    ssm_s = ssm[:, 1:3].reshape(1, 16, 32, 64, 128)
    cv = cat("o_conv")
    conv_p = cv[:, 0:3].reshape(1, B, 3, 3072)
    conv_s = cv[:, 3:9].reshape(1, 16, 3, 3072)
    bk_p = cat("o_bk_p").reshape(1, B, 512, 32, 64)
    bv_p = cat("o_bv_p").reshape(1, B, 512, 32, 64)
    bk_s = cat("o_bk_s").reshape(1, 16, 512, 32, 64)
    bv_s = cat("o_bv_s").reshape(1, 16, 512, 32, 64)
    return (y_p, y_s, sbk_p, sbv_p, ssm_p, conv_p, bk_p, bv_p,
            sbk_s, sbv_s, ssm_s, conv_s, bk_s, bv_s)
```

```python
from contextlib import ExitStack
import numpy as np
import concourse.bass as bass
import concourse.mybir as mybir
from concourse.bass_utils import run_bass_kernel_spmd

F32 = mybir.dt.float32
BF16 = mybir.dt.bfloat16
F32R = mybir.dt.float32r
AF = mybir.ActivationFunctionType
ALU = mybir.AluOpType
AX = mybir.AxisListType

NCORES = 8
D = 2048
KC = 16
SEQ = 2048
NT = 17
TOK = NT * 128
L0_IN = 9248
SEG = 30000


class _Scope:
    def __init__(self, S):
        self.S = S

    def __enter__(self):
        self.prev = self.S.stack
        self.st = ExitStack()
        self.S.stack = self.st
        return self

    def __exit__(self, *a):
        self.S.barrier()
        self.st.close()
        self.S.stack = self.prev
        return False


class Region:
    __slots__ = ("w", "r")

    def __init__(self):
        self.w = None
        self.r = {}


class Ring:
    def __init__(self, tiles):
        self.t = tiles
        self.R = [Region() for _ in tiles]
        self.i = -1

    def next(self):
        self.i = (self.i + 1) % len(self.t)
        return self.t[self.i], self.R[self.i]


class Sched:
    ENGS = ("pe", "act", "dve", "pool", "sp")

    def __init__(self, nc, stack, dma_slots=8):
        self.nc = nc
        self.root = stack
        self.stack = stack
        self.e = {"pe": nc.tensor, "act": nc.scalar, "dve": nc.vector,
                  "pool": nc.gpsimd, "sp": nc.sync}
        self.sems = {}
        self.cnt = {k: 0 for k in self.ENGS}
        self.waited = {k: {} for k in self.ENGS}
        self.dma_slots = dma_slots
        self.dq = {}
        self.ninst = 0
        self.nwait = 0
        self._n = 0

    def sbuf(self, shape, dt, name=None):
        self._n += 1
        return self.stack.enter_context(
            self.nc.sbuf_tensor("%s_%d" % (name or "sb", self._n), list(shape), dt))

    def psum(self, shape, dt, name=None):
        self._n += 1
        return self.stack.enter_context(
            self.nc.psum_tensor("%s_%d" % (name or "ps", self._n), list(shape), dt))

    def dram(self, name, shape, dt, kind="Internal"):
        return self.nc.dram_tensor(name, list(shape), dt, kind=kind).ap()

    def _sem(self, key):
        s = self.sems.get(key)
        if s is None:
            s = self.root.enter_context(self.nc.semaphore("s_" + key.replace("#", "_")))
            self.sems[key] = s
        return s

    def _emit_waits(self, eng, need):
        w = self.waited[eng]
        for key, val in need.items():
            if val <= 0 or w.get(key, 0) >= val:
                continue
            if eng == "pe" and key.startswith("pe#"):
                continue
            self.e[eng].wait_ge(self._sem(key), val)
            self.nwait += 1
            w[key] = val

    @staticmethod
    def _need(reads, writes):
        need = {}
        for R in reads:
            if R.w is not None:
                k, v = R.w
                if need.get(k, 0) < v:
                    need[k] = v
        for R in writes:
            if R.w is not None:
                k, v = R.w
                if need.get(k, 0) < v:
                    need[k] = v
            for k, v in R.r.items():
                if need.get(k, 0) < v:
                    need[k] = v
        return need

    @staticmethod
    def _mark(tok, reads, writes):
        k, v = tok
        for R in reads:
            if R.r.get(k, 0) < v:
                R.r[k] = v
        for R in writes:
            R.w = tok
            R.r = {}

    def op(self, eng, fn, reads=(), writes=()):
        self._emit_waits(eng, self._need(reads, writes))
        inst = fn(self.e[eng])
        c = self.cnt[eng]
        self.cnt[eng] = c + 1
        key = "%s#%d" % (eng, c // SEG)
        inst.then_inc(self._sem(key), 1)
        self._mark((key, c % SEG + 1), reads, writes)
        self.ninst += 1
        return inst

    def dma(self, queue, out, in_, reads=(), writes=(), **kw):
        q = self.dq.get(queue)
        if q is None:
            q = {"n": 0, "uses": [0] * self.dma_slots,
                 "keys": ["d%s%d" % (queue, i) for i in range(self.dma_slots)]}
            self.dq[queue] = q
        slot = q["n"] % self.dma_slots
        q["n"] += 1
        key = q["keys"][slot]
        need = self._need(reads, writes)
        prev = 16 * q["uses"][slot]
        if prev > 0 and need.get(key, 0) < prev:
            need[key] = prev
        w = self.waited[queue]
        for k, v in need.items():
            if v <= 0 or w.get(k, 0) >= v:
                continue
            self.e[queue].wait_ge(self._sem(k), v)
            self.nwait += 1
            w[k] = v
        inst = self.e[queue].dma_start(out=out, in_=in_, **kw)
        q["uses"][slot] += 1
        inst.then_inc(self._sem(key), 16)
        self._mark((key, 16 * q["uses"][slot]), reads, writes)
        self.ninst += 1
        return inst

    def scope(self):
        return _Scope(self)

    def _all_tokens(self):
        need = {}
        for k in self.ENGS:
            c = self.cnt[k]
            if c > 0:
                need["%s#%d" % (k, (c - 1) // SEG)] = (c - 1) % SEG + 1
        for q in self.dq.values():
            for i, key in enumerate(q["keys"]):
                if q["uses"][i]:
                    need[key] = 16 * q["uses"][i]
        return need

    def barrier(self, engines=None):
        need = self._all_tokens()
        for eng in (engines or self.ENGS):
            w = self.waited[eng]
            for k, v in need.items():
                if w.get(k, 0) >= v:
                    continue
                self.e[eng].wait_ge(self._sem(k), v)
                self.nwait += 1
                w[k] = v


DEBUG = False
SKIP_SB = False
SKIP_REST = False
TG = [(0, 512), (512, 512), (1024, 512), (1536, 512), (2048, 128)]


def build_program():
    nc = bass.Bass("TRN2", target_bir_lowering=False)
    st = ExitStack()
    S = Sched(nc, st)

    def din(name, shape):
        return nc.dram_tensor(name, list(shape), F32, kind="ExternalInput").ap()

    def dout(name, shape, dt=F32):
        return nc.dram_tensor(name, list(shape), dt, kind="ExternalOutput").ap()

    def dscr(name, shape, dt):
        kind = "ExternalOutput" if DEBUG else "Internal"
        return nc.dram_tensor(name, list(shape), dt, kind=kind).ap()

    xp = din("xp", [SEQ, D])
    xs = din("xs", [128, D])
    csk = din("csk", [2, 4096, 1024])
    csv = din("csv", [2, 4096, 1024])
    w_in0 = din("w_in0", [D, L0_IN])
    sssm = din("sssm", [2, 2048, 128])
    sconv = din("sconv", [2, 3, 3072])
    conv_w = din("conv_w", [4, 3072])
    conv_b = din("conv_b", [3072])
    dt_bias = din("dt_bias", [32])
    a_log = din("a_log", [32])
    d_skip = din("d_skip", [32])
    norm_w = din("norm_w", [2048])
    w_out0 = din("w_out0", [3072, 2048])
    ln_g0 = din("ln_g0", [2048])
    ln_b0 = din("ln_b0", [2048])
    cbk = din("cbk", [2, 512, 2048])
    cbv = din("cbv", [2, 512, 2048])
    w_in1 = din("w_in1", [D, 8192])
    rel_bias = din("rel_bias", [32, 257])
    w_out1 = din("w_out1", [2048, 2048])
    ln_g1 = din("ln_g1", [2048])
    ln_b1 = din("ln_b1", [2048])
    o_sbk_p = dout("o_sbk_p", [SEQ, 1024])
    o_sbv_p = dout("o_sbv_p", [SEQ, 1024])
    o_sbk_s = dout("o_sbk_s", [128, 1024])
    o_sbv_s = dout("o_sbv_s", [128, 1024])
    o_conv = dout("o_conv", [9, 3072])
    o_ssm = dout("o_ssm", [3, 2048, 128])
    o_yp = dout("o_yp", [SEQ, D])
    o_ys = dout("o_ys", [128, D])
    o_bk_p = dout("o_bk_p", [512, 2048])
    o_bv_p = dout("o_bv_p", [512, 2048])
    o_bk_s = dout("o_bk_s", [2, 512, 2048])
    o_bv_s = dout("o_bv_s", [2, 512, 2048])
    oT0 = dscr("oT0", [24, 128, TOK], BF16)
    oT0_R = [Region() for _ in range(24)]

    constR = Region()
    ident_b = S.sbuf([128, 128], BF16, "ident_b")
    ident_f = S.sbuf([128, 128], F32, "ident_f")
    ones_b = S.sbuf([128, 512], BF16, "ones_b")
    ones_f = S.sbuf([128, 128], F32, "ones_f")
    uincl = S.sbuf([128, 128], BF16, "uincl")
    S.op("pool", lambda e: e.memset(ones_b[:], 1.0), writes=[constR])
    zeros_f = S.sbuf([128, 256], F32, "zeros_f")
    ones_src = S.sbuf([128, 128], F32, "ones_src")
    S.op("pool", lambda e: e.memset(zeros_f[:], 0.0), writes=[constR])
    S.op("pool", lambda e: e.memset(ones_src[:], 1.0), writes=[constR])
    S.op("pool", lambda e: e.tensor_copy(out=ones_f[:].bitcast(F32R), in_=ones_src[:]),
         writes=[constR])
    S.op("pool", lambda e: e.memset(ident_b[:], 0.0), writes=[constR])
    S.op("pool", lambda e: e.memset(ident_f[:], 0.0), writes=[constR])
    for idt in (ident_b, ident_f):
        S.op("pool", lambda e: e.affine_select(
            out=idt[:], in_=idt[:], pattern=[[-1, 128]], compare_op=ALU.not_equal,
            fill=1.0, base=0, channel_multiplier=1), writes=[constR])
    S.op("pool", lambda e: e.affine_select(
        out=uincl[:], in_=ones_b[:, 0:128], pattern=[[-1, 128]], compare_op=ALU.is_ge,
        fill=0.0, base=0, channel_multiplier=1), writes=[constR])

    class _XT:
        def __init__(self):
            self.t = None
        def __getitem__(self, k):
            return self.t[k]
    xT = _XT()
    xT_R = [Region() for _ in range(NT)]

    def x_rows(src_p, src_s, t):
        return src_p[t * 128:(t + 1) * 128, :] if t < 16 else src_s[:, :]

    def build_xT(src_p, src_s, srcR=None):
        with S.scope():
            xb = Ring([S.sbuf([128, D], BF16, "xb%d" % i) for i in range(2)])
            pT = Ring([S.psum([128, 8, 128], BF16, "pT%d" % i) for i in range(2)])
            for t in range(NT):
                xbt, xbR = xb.next()
                S.dma("pool", xbt[:], x_rows(src_p, src_s, t), reads=([srcR] if srcR else []),
                      writes=[xbR])
                for g in range(2):
                    pt, pR = pT.next()
                    for j in range(8):
                        kc = g * 8 + j
                        S.op("pe", lambda e: e.transpose(
                            pt[:, j, :], xbt[:, kc * 128:(kc + 1) * 128], ident_b[:]),
                            reads=[xbR, constR], writes=[pR])
                    dst = xT[:, g * 8:(g + 1) * 8, t * 128:(t + 1) * 128]
                    if g == 0:
                        S.op("act", lambda e: e.copy(out=dst, in_=pt[:]), reads=[pR],
                             writes=[xT_R[t]])
                    else:
                        S.op("dve", lambda e: e.tensor_copy(out=dst, in_=pt[:]), reads=[pR],
                             writes=[xT_R[t]])

    def sb_pipeline(R, tiles):
        ss_prev = {}

        def S0(T):
            blk = T["blk"]
            nk = blk["nk"]
            zt, ztR = R["zt"].next()
            for sg, (kap, vap, kvR) in zip(T["segs"], blk["kv"]):
                S.op("pe", lambda e: e.matmul(zt[0:nk, sg["c0"]:sg["c0"] + sg["n"]], lhsT=kap,
                                              rhs=sg["qs"], start=True, stop=True),
                     reads=[kvR, sg["qR"]], writes=[ztR])
            T["zt"] = (zt, ztR)

        def S1(T):
            blk = T["blk"]
            nk, mask, ncols = blk["nk"], blk["mask"], T["ncols"]
            zt, ztR = T["zt"]
            et, etR = R["e"].next()
            S.op("act", lambda e: e.activation(out=et[0:nk, 0:ncols], in_=zt[0:nk, 0:ncols],
                                               func=AF.Exp), reads=[ztR], writes=[etR])
            spt, spR = R["sp"].next()
            S.op("act", lambda e: e.activation(out=spt[0:nk, 0:ncols], in_=et[0:nk, 0:ncols],
                                               func=AF.Ln, bias=1.0, scale=1.0),
                 reads=[etR], writes=[spR])
            if mask is not None:
                S.op("dve", lambda e: e.tensor_tensor(out=spt[0:nk, 0:ncols],
                                                      in0=spt[0:nk, 0:ncols], in1=mask,
                                                      op=ALU.mult),
                     reads=[spR, constR], writes=[spR])
            prev = ss_prev.get(T["chain"])
            T["sp"] = (spt, spR)
            T["e"] = (et, etR)
            T["ss_in"] = prev
            if not T["last"]:
                nst, nsR = R["ss"].next()
                if prev is None:
                    S.op("pool", lambda e: e.tensor_copy(out=nst[0:nk, 0:ncols].bitcast(F32R),
                                                         in_=spt[0:nk, 0:ncols]),
                         reads=[spR], writes=[nsR])
                else:
                    sst, ssR = prev
                    S.op("pool", lambda e: e.tensor_tensor(out=nst[0:nk, 0:ncols].bitcast(F32R),
                                                           in0=sst[0:nk, 0:ncols],
                                                           in1=spt[0:nk, 0:ncols], op=ALU.add),
                         reads=[ssR, spR], writes=[nsR])
                ss_prev[T["chain"]] = (nst, nsR)

        def S2(T):
            blk = T["blk"]
            nk, ncols = blk["nk"], T["ncols"]
            spt, spR = T["sp"]
            tt, ttR = R["t"].next()
            S.op("pe", lambda e: e.matmul(tt[0:nk, 0:ncols], lhsT=uincl[0:nk, 0:nk],
                                          rhs=spt[0:nk, 0:ncols], start=True,
                                          stop=(T["ss_in"] is None)),
                 reads=[spR, constR], writes=[ttR])
            if T["ss_in"] is not None:
                sst, ssR = T["ss_in"]
                S.op("pe", lambda e: e.matmul(tt[:, 0:ncols], lhsT=ones_f[:].bitcast(F32R),
                                              rhs=sst[:, 0:ncols].bitcast(F32R), start=False,
                                              stop=True),
                     reads=[ssR, constR], writes=[ttR])
            T["tt"] = (tt, ttR)

        def S3(T):
            blk = T["blk"]
            nk, mask, ncols = blk["nk"], blk["mask"], T["ncols"]
            tt, ttR = T["tt"]
            ut, uR = R["u"].next()
            S.op("act", lambda e: e.activation(out=ut[0:nk, 0:ncols], in_=tt[0:nk, 0:ncols],
                                               func=AF.Exp, scale=-1.0),
                 reads=[ttR], writes=[uR])
            et, etR = T["e"]
            wt_, wR_ = R["w"].next()
            S.op("dve", lambda e: e.tensor_tensor(out=wt_[0:nk, 0:ncols], in0=et[0:nk, 0:ncols],
                                                  in1=ut[0:nk, 0:ncols], op=ALU.mult),
                 reads=[etR, uR], writes=[wR_])
            if mask is not None:
                S.op("dve", lambda e: e.tensor_tensor(out=wt_[0:nk, 0:ncols],
                                                      in0=wt_[0:nk, 0:ncols], in1=mask,
                                                      op=ALU.mult),
                     reads=[wR_, constR], writes=[wR_])
            T["w"] = (wt_, wR_)

        def S4(T):
            blk = T["blk"]
            nk = blk["nk"]
            wt_, wR_ = T["w"]
            for sg, (kap, vap, kvR) in zip(T["segs"], blk["kv"]):
                S.op("pe", lambda e: e.matmul(sg["o"], lhsT=vap,
                                              rhs=wt_[0:nk, sg["c0"]:sg["c0"] + sg["n"]],
                                              start=(T["first"] and sg.get("ost", True)),
                                              stop=T["last"],
                                              skip_group_check=(not sg.get("ost", True))),
                     reads=[kvR, wR_], writes=[sg["oR"]])
            if T.get("after") is not None:
                T["after"]()

        stages = (S0, S1, S2, S3, S4)
        n = len(tiles)
        for step in range(n + 4):
            for lag in (4, 3, 2, 1, 0):
                idx = step - lag
                if 0 <= idx < n:
                    stages[lag](tiles[idx])

    def run_interleaved(gens):
        gens = list(gens)
        while gens:
            for g in list(gens):
                try:
                    next(g)
                except StopIteration:
                    gens.remove(g)

    def layer0_sb():
        W0v = w_in0.rearrange("(kc p) n -> p kc n", p=128)
        with S.scope():
            wA = Ring([S.sbuf([128, KC, 4, 128], BF16, "wA%d" % i) for i in range(2)])
            kv_st = Ring([S.sbuf([128, 4, 256], F32, "kv_st%d" % i) for i in range(2)])
            qTs = S.sbuf([128, TOK], BF16, "qTs")
            nqT = qTs
            kT = S.sbuf([128, TOK], BF16, "kT")
            sgT = S.sbuf([128, TOK], BF16, "sgT")
            v_p = S.sbuf([128, NT, 128], BF16, "v_p")
            qR, kR, gR, vR = Region(), Region(), Region(), Region()
            kst = S.sbuf([128, 16, 128], F32, "kst")
            kstR = Region()
            kTp = S.sbuf([128, 2, 4224], BF16, "kTp")
            kTpR = Region()
            vpast = S.sbuf([128, 2, 33, 128], BF16, "vpast")
            vpR = Region()
            S.op("pool", lambda e: e.memset(kTp[:, :, 4096:4224], 0.0), writes=[kTpR])
            S.op("pool", lambda e: e.memset(vpast[:, :, 32, :], 0.0), writes=[vpR])
            mdiag = S.sbuf([128, 4, 512], BF16, "mdiag")
            msamp = S.sbuf([128, 4, 64], BF16, "msamp")
            for dj in range(4):
                S.op("pool", lambda e: e.affine_select(
                    out=mdiag[:, dj, :], in_=ones_b[:, :], pattern=[[1, 512]],
                    compare_op=ALU.is_gt, fill=0.0, base=-128 * dj, channel_multiplier=-1),
                    reads=[constR], writes=[constR])
            S.op("pool", lambda e: e.affine_select(
                out=msamp[:], in_=ones_b[:, 0:256].rearrange("p (a b) -> p a b", a=4),
                pattern=[[0, 4], [1, 64]], compare_op=ALU.is_gt, fill=0.0, base=0,
                channel_multiplier=-1), reads=[constR], writes=[constR])
            R = {
                "zt": Ring([S.psum([128, 512], F32, "zt%d" % i) for i in range(2)]),
                "t": Ring([S.psum([128, 512], F32, "tt%d" % i) for i in range(2)]),
                "e": Ring([S.sbuf([128, 512], F32, "e%d" % i) for i in range(4)]),
                "u": Ring([S.sbuf([128, 512], BF16, "u%d" % i) for i in range(2)]),
                "sp": Ring([S.sbuf([128, 512], BF16, "sp%d" % i) for i in range(3)]),
                "ss": Ring([S.sbuf([128, 512], F32, "ss%d" % i) for i in range(6)]),
                "w": Ring([S.sbuf([128, 512], BF16, "w%d" % i) for i in range(3)]),
            }
            obank = Ring([S.psum([128, 512], F32, "ob%d" % i) for i in range(2)])
            pproj = Ring([S.psum([128, 512], F32, "pp%d" % i) for i in range(2)])
            ogs = Ring([S.sbuf([128, 512], BF16, "og%d" % i) for i in range(2)])

            def load_wA(p):
                wt, wR = wA.next()
                for s in range(4):
                    c0 = s * 1024 + p * 128
                    S.dma("pool", wt[:, :, s, :], W0v[:, :, c0:c0 + 128], writes=[wR])
                return wt, wR

            def proj(p, wt, wR):
                for s in (0, 1, 3):
                    for (t0, n) in TG:
                        ps, psR = pproj.next()
                        for kc in range(KC):
                            S.op("pe", lambda e: e.matmul(
                                ps[:, 0:n], lhsT=wt[:, kc, s, :], rhs=xT[:, kc, t0:t0 + n],
                                start=(kc == 0), stop=(kc == KC - 1)),
                                reads=[wR] + xT_R[t0 // 128:(t0 + n) // 128], writes=[psR])
                        if s == 0:
                            S.op("dve", lambda e: e.tensor_scalar(
                                out=qTs[:, t0:t0 + n], in0=ps[:, 0:n], scalar1=0.125, scalar2=None,
                                op0=ALU.mult), reads=[psR], writes=[qR])
                        elif s == 1:
                            S.op("dve", lambda e: e.tensor_copy(out=kT[:, t0:t0 + n],
                                                                in_=ps[:, 0:n]),
                                 reads=[psR], writes=[kR])
                        else:
                            S.op("act", lambda e: e.activation(out=sgT[:, t0:t0 + n],
                                                               in_=ps[:, 0:n], func=AF.Silu),
                                 reads=[psR], writes=[gR])
                    yield
                for t0 in range(0, NT, 4):
                    tl = list(range(t0, min(t0 + 4, NT)))
                    stg, stR = kv_st.next()
                    for i, t in enumerate(tl):
                        ps, psR = pproj.next()
                        for kc in range(KC):
                            S.op("pe", lambda e: e.matmul(
                                ps[:, 0:256], lhsT=xT[:, kc, t * 128:(t + 1) * 128],
                                rhs=wt[:, kc, 1:3, :], start=(kc == 0), stop=(kc == KC - 1)),
                                reads=[xT_R[t], wR], writes=[psR])
                        S.op("dve", lambda e: e.tensor_copy(out=stg[:, i, :], in_=ps[:, 0:256]),
                             reads=[psR], writes=[stR])
                        S.op("pool", lambda e: e.tensor_copy(out=v_p[:, t, :],
                                                             in_=stg[:, i, 128:256]),
                             reads=[stR], writes=[vR])
                    pt_tiles = [t for t in tl if t < 16]
                    if pt_tiles:
                        n = len(pt_tiles)
                        r0 = pt_tiles[0] * 128
                        for s_, dst in ((0, o_sbk_p), (1, o_sbv_p)):
                            S.dma("sp",
                                  dst[r0:r0 + n * 128, p * 128:(p + 1) * 128].rearrange(
                                      "(t q) c -> q t c", q=128),
                                  stg[:, 0:n, s_ * 128:(s_ + 1) * 128], reads=[stR])
                    if 16 in tl:
                        i = tl.index(16)
                        for s_, dst in ((0, o_sbk_s), (1, o_sbv_s)):
                            S.dma("sp", dst[:, p * 128:(p + 1) * 128],
                                  stg[:, i, s_ * 128:(s_ + 1) * 128], reads=[stR])
                    yield

            def load_sample_kv(p):
                for b in range(2):
                    S.dma("pool", vpast[:, b, 0:32, :],
                          csv[b].rearrange("(blk q) c -> q blk c", q=128)[:, :, p * 128:(p + 1) * 128],
                          writes=[vpR])
                for b in range(2):
                    for half in range(2):
                        S.dma("sp", kst[:],
                              csk[b, half * 2048:(half + 1) * 2048, :].rearrange(
                                  "(blk q) c -> q blk c", q=128)[:, :, p * 128:(p + 1) * 128],
                              writes=[kstR])
                        for g in range(4):
                            ps, psR = pproj.next()
                            for j in range(4):
                                S.op("pe", lambda e: e.transpose(
                                    ps[:, j * 128:(j + 1) * 128], kst[:, g * 4 + j, :], ident_f[:]),
                                    reads=[kstR, constR], writes=[psR])
                            c0 = half * 2048 + g * 512
                            S.op("dve", lambda e: e.tensor_copy(out=kTp[:, b, c0:c0 + 512],
                                                                in_=ps[:]),
                                 reads=[psR], writes=[kTpR])
                        yield

            def attn(p):
                tiles = []
                for Q in range(4):
                    ob, obR = obank.next()
                    per_head = []
                    for hh in range(2):
                        hs = slice(hh * 64, hh * 64 + 64)
                        seg = dict(c0=0, n=512, hh=hh, qs=qTs[hs, Q * 512:(Q + 1) * 512],
                                   nqs=nqT[hs, Q * 512:(Q + 1) * 512], qR=qR,
                                   o=ob[hs, :], oR=obR)
                        tl = []
                        js = list(range(4 * Q + 3, -1, -1))
                        for bi, j in enumerate(js):
                            dj = j - 4 * Q
                            blk = dict(nk=128, mask=(mdiag[:, dj, :] if dj >= 0 else None),
                                       kv=[(kT[hs, j * 128:(j + 1) * 128], v_p[:, j, hs], kvR_all)])
                            tl.append(dict(segs=[seg], blk=blk, ncols=512, chain=(p, Q, hh),
                                           first=(bi == 0), last=(bi == len(js) - 1), after=None))
                        per_head.append(tl)

                    def fin(ob=ob, obR=obR, Q=Q):
                        og, ogR = ogs.next()
                        S.op("dve", lambda e: e.tensor_tensor(
                            out=og[:], in0=ob[:], in1=sgT[:, Q * 512:(Q + 1) * 512], op=ALU.mult),
                            reads=[obR, gR], writes=[ogR])
                        S.dma("sp", oT0[p, :, Q * 512:(Q + 1) * 512], og[:], reads=[ogR],
                              writes=[oT0_R[p]])
                    per_head[1][-1]["after"] = fin
                    for t0_, t1_ in zip(per_head[0], per_head[1]):
                        tiles.append(t0_)
                        tiles.append(t1_)
                ob, obR = obank.next()
                per_head = []
                for hh in range(2):
                    hs = slice(hh * 64, hh * 64 + 64)
                    segs = []
                    for b in range(2):
                        tq = slice(2048 + 64 * b, 2048 + 64 * b + 64)
                        segs.append(dict(c0=b * 64, n=64, hh=hh, b=b, qs=qTs[hs, tq],
                                         nqs=nqT[hs, tq], qR=qR, ost=(b == 0),
                                         o=ob[hs, b * 64:(b + 1) * 64], oR=obR))
                    tl = []
                    for bi, j in enumerate(range(32, -1, -1)):
                        kv = []
                        for sg in segs:
                            kv.append((kTp[hs, sg["b"], j * 128:j * 128 + 128],
                                       vpast[0:128, sg["b"], j, hs], kvR_all))
                        blk = dict(nk=128, kv=kv, mask=(
                            msamp[:, 0:2, :].rearrange("p a b -> p (a b)") if j == 32 else None))
                        tl.append(dict(segs=segs, blk=blk, ncols=128, chain=(p, 9, hh),
                                       first=(bi == 0), last=(bi == 32), after=None))
                    per_head.append(tl)

                def fin_s(ob=ob, obR=obR):
                    og, ogR = ogs.next()
                    S.op("dve", lambda e: e.tensor_tensor(
                        out=og[:, 0:128], in0=ob[:, 0:128], in1=sgT[:, 2048:2176], op=ALU.mult),
                        reads=[obR, gR], writes=[ogR])
                    S.dma("sp", oT0[p, :, 2048:2176], og[:, 0:128], reads=[ogR], writes=[oT0_R[p]])
                per_head[1][-1]["after"] = fin_s
                for t0_, t1_ in zip(per_head[0], per_head[1]):
                    tiles.append(t0_)
                    tiles.append(t1_)
                sb_pipeline(R, tiles)

            kvR_all = Region()
            dummy = S.sbuf([128, 8], F32, "dmy_sync")

            npairs = NPAIRS_DBG if DEBUG else 8
            nxt = load_wA(0)
            for p in range(npairs):
                wt, wR = nxt
                if p + 1 < npairs:
                    nxt = load_wA(p + 1)
                for _ in proj(p, wt, wR):
                    pass
                for _ in load_sample_kv(p):
                    pass
                for b in range(2):
                    S.op("pool", lambda e: e.tensor_copy(
                        out=kTp[:, b, 4096:4160], in_=kT[:, 2048 + 64 * b:2048 + 64 * b + 64]),
                        reads=[kR], writes=[kTpR])
                    S.dma("sp", vpast[0:64, b, 32, :], v_p[64 * b:64 * b + 64, 16, :],
                          reads=[vR], writes=[vpR])
                S.op("pool", lambda e: e.memset(dummy[:, 0:1], 0.0),
                     reads=[kR, vR, kTpR, vpR], writes=[kvR_all])
                attn(p)
                S.op("pool", lambda e: e.memset(dummy[:, 0:1], 0.0),
                     reads=[kvR_all], writes=[kR, vR, kTpR, vpR, qR, gR])

    zs = dscr("zs", [TOK, 2048], F32)
    zs_R = Region()
    xbcT = dscr("xbcT", [24, 128, TOK], BF16)
    xbcT_R = Region()
    dt_raw = S.sbuf([128, NT, 32], F32, "dt_raw")
    dtR = Region()

    def layer0_ssd_proj():
        W0v = w_in0.rearrange("(kc p) n -> p kc n", p=128)
        with S.scope():
            wB = Ring([S.sbuf([128, KC, 512], BF16, "wB%d" % i) for i in range(2)])
            wD = S.sbuf([128, KC, 32], BF16, "wD")
            wDR = Region()
            pp = Ring([S.psum([128, 512], F32, "sp_pp%d" % i) for i in range(4)])
            zst = Ring([S.sbuf([128, 512], F32, "zst%d" % i) for i in range(3)])
            xst = Ring([S.sbuf([128, TOK], BF16, "xst%d" % i) for i in range(2)])
            cst = Ring([S.sbuf([128, 512], F32, "cst%d" % i) for i in range(2)])
            ev = [0]

            def evac(out, in_, reads, writes, func=None):
                ev[0] += 1
                if func is not None:
                    S.op("act", lambda e: e.activation(out=out, in_=in_, func=func), reads=reads,
                         writes=writes)
                elif ev[0] % 2:
                    S.op("dve", lambda e: e.tensor_copy(out=out, in_=in_), reads=reads,
                         writes=writes)
                else:
                    S.op("act", lambda e: e.copy(out=out, in_=in_), reads=reads, writes=writes)

            def load_wB(c0):
                wt, wR = wB.next()
                S.dma("pool", wt[:], W0v[:, :, c0:c0 + 512], writes=[wR])
                return wt, wR

            S.dma("pool", wD[:], W0v[:, :, 9216:9248], writes=[wDR])
            blocks = [("z", 4096 + i * 512, i) for i in range(4)] + \
                     [("x", 6144 + i * 512, i) for i in range(6)]
            nxt = load_wB(blocks[0][1])
            for bi, (kind, c0, i) in enumerate(blocks):
                wt, wR = nxt
                if bi + 1 < len(blocks):
                    nxt = load_wB(blocks[bi + 1][1])
                if kind == "z":
                    for t in range(NT):
                        ps, psR = pp.next()
                        for kc in range(KC):
                            S.op("pe", lambda e: e.matmul(
                                ps[:], lhsT=xT[:, kc, t * 128:(t + 1) * 128], rhs=wt[:, kc, :],
                                start=(kc == 0), stop=(kc == KC - 1)),
                                reads=[xT_R[t], wR], writes=[psR])
                        zt_, ztR_ = zst.next()
                        evac(zt_[:], ps[:], [psR], [ztR_], func=AF.Silu)
                        S.dma("sp", zs[t * 128:(t + 1) * 128, i * 512:(i + 1) * 512], zt_[:],
                              reads=[ztR_], writes=[zs_R])
                else:
                    for t in (15, 16):
                        ps, psR = pp.next()
                        for kc in range(KC):
                            S.op("pe", lambda e: e.matmul(
                                ps[:], lhsT=xT[:, kc, t * 128:(t + 1) * 128], rhs=wt[:, kc, :],
                                start=(kc == 0), stop=(kc == KC - 1)),
                                reads=[xT_R[t], wR], writes=[psR])
                        ct_, cR_ = cst.next()
                        evac(ct_[:], ps[:], [psR], [cR_])
                        cs = slice(i * 512, (i + 1) * 512)
                        if t == 15:
                            S.dma("sp", o_conv[0:3, cs], ct_[125:128, :], reads=[cR_])
                        else:
                            S.dma("sp", o_conv[3:6, cs], ct_[61:64, :], reads=[cR_])
                            S.dma("sp", o_conv[6:9, cs], ct_[125:128, :], reads=[cR_])
                    for c4 in range(4):
                        cc = i * 4 + c4
                        xt_, xR_ = xst.next()
                        for (t0, n) in TG:
                            ps, psR = pp.next()
                            for kc in range(KC):
                                S.op("pe", lambda e: e.matmul(
                                    ps[:, 0:n], lhsT=wt[:, kc, c4 * 128:(c4 + 1) * 128],
                                    rhs=xT[:, kc, t0:t0 + n], start=(kc == 0), stop=(kc == KC - 1)),
                                    reads=[wR] + xT_R[t0 // 128:(t0 + n) // 128], writes=[psR])
                            evac(xt_[:, t0:t0 + n], ps[:, 0:n], [psR], [xR_])
                        S.dma("sp", xbcT[cc], xt_[:], reads=[xR_], writes=[xbcT_R])
            for t in range(NT):
                ps, psR = pp.next()
                for kc in range(KC):
                    S.op("pe", lambda e: e.matmul(
                        ps[:, 0:32], lhsT=xT[:, kc, t * 128:(t + 1) * 128], rhs=wD[:, kc, :],
                        start=(kc == 0), stop=(kc == KC - 1)),
                        reads=[xT_R[t], wDR], writes=[psR])
                evac(dt_raw[:, t, :], ps[:, 0:32], [psR], [dtR])

    def ssd_phase():
        with S.scope():
            cR = Region()
            cwT = S.sbuf([128, 24, 4], F32, "cwT")
            cbT = S.sbuf([128, 24], F32, "cbT")
            cb_row = S.sbuf([1, 2560], BF16, "cb_row")
            dtb_bc = S.sbuf([128, 32], F32, "dtb_bc")
            a_bc = S.sbuf([128, 32], F32, "a_bc")
            d_bc = S.sbuf([128, 32], F32, "d_bc")
            nw_bc = S.sbuf([128, 2048], F32, "nw_bc")
            with nc.allow_non_contiguous_dma(reason="tiny transposed parameter loads"):
                for j in range(4):
                    S.dma("sp", cwT[:, :, j], conv_w[j].rearrange("(cc p) -> p cc", p=128),
                          writes=[cR])
                S.dma("sp", cbT[:], conv_b.rearrange("(cc p) -> p cc", p=128), writes=[cR])
            for i5 in range(5):
                S.dma("pool", cb_row[:, i5 * 512:(i5 + 1) * 512],
                      conv_b[i5 * 512:(i5 + 1) * 512].rearrange("(o n) -> o n", o=1), writes=[cR])
            S.dma("sp", dtb_bc[:], dt_bias.partition_broadcast(128), writes=[cR])
            S.dma("sp", a_bc[:], a_log.partition_broadcast(128), writes=[cR])
            S.dma("sp", d_bc[:], d_skip.partition_broadcast(128), writes=[cR])
            S.dma("sp", nw_bc[:], norm_w.partition_broadcast(128), writes=[cR])
            S.op("act", lambda e: e.activation(out=a_bc[:], in_=a_bc[:], func=AF.Exp),
                 reads=[cR], writes=[cR])
            S.op("dve", lambda e: e.tensor_scalar(out=a_bc[:], in0=a_bc[:], scalar1=-1.0,
                                                  scalar2=None, op0=ALU.mult),
                 reads=[cR], writes=[cR])
            diagw = S.sbuf([128, 24, 4, 128], BF16, "diagw")
            for cc in range(24):
                for j in range(4):
                    S.op("dve", lambda e: e.tensor_scalar(
                        out=diagw[:, cc, j, :], in0=ident_b[:], scalar1=cwT[:, cc, j:j + 1],
                        scalar2=None, op0=ALU.mult), reads=[cR, constR], writes=[cR])
            Dd = S.sbuf([128, 32, 128], BF16, "Dd")
            for h in range(32):
                S.op("dve", lambda e: e.tensor_scalar(
                    out=Dd[:, h, :], in0=ident_b[:], scalar1=d_bc[:, h:h + 1], scalar2=None,
                    op0=ALU.mult), reads=[cR, constR], writes=[cR])
            ones128 = S.sbuf([128, 128], F32, "ones128")
            Mtri = S.sbuf([128, 128], F32, "Mtri")
            Tcum = S.sbuf([128, 128], F32, "Tcum")
            Bones = S.sbuf([128, 128], F32, "Bones")
            Cones = S.sbuf([128, 2, 128], F32, "Cones")
            LT = S.sbuf([128, 64], F32, "LT")
            BDm = S.sbuf([128, 128], F32, "BDm")
            S.op("pool", lambda e: e.memset(ones128[:], 1.0), writes=[cR])
            S.op("pool", lambda e: e.affine_select(
                out=Mtri[:], in_=ones128[:], pattern=[[-1, 128]], compare_op=ALU.is_gt, fill=0.0,
                base=0, channel_multiplier=1), reads=[cR], writes=[cR])
            S.op("pool", lambda e: e.affine_select(
                out=Tcum[:], in_=ones128[:], pattern=[[1, 128]], compare_op=ALU.is_ge, fill=0.0,
                base=0, channel_multiplier=-1), reads=[cR], writes=[cR])
            S.op("pool", lambda e: e.memset(Bones[:], 1.0), writes=[cR])
            S.op("pool", lambda e: e.memset(Cones[:], 0.0), writes=[cR])
            S.op("pool", lambda e: e.memset(Cones[0:64, 0, :], 1.0), writes=[cR])
            S.op("pool", lambda e: e.memset(Cones[64:128, 1, :], 1.0), writes=[cR])
            for m in (Mtri, Tcum, Bones):
                S.op("pool", lambda e: e.memset(m[0:64, 64:128], 0.0), writes=[cR])
                S.op("pool", lambda e: e.memset(m[64:128, 0:64], 0.0), writes=[cR])
            for hb in (0, 64):
                S.op("pool", lambda e: e.affine_select(
                    out=LT[hb:hb + 64, :], in_=ones128[hb:hb + 64, 0:64], pattern=[[1, 64]],
                    compare_op=ALU.is_ge, fill=0.0, base=0, channel_multiplier=-1),
                    reads=[cR], writes=[cR])
            S.op("pool", lambda e: e.tensor_copy(out=BDm[:], in_=Tcum[:]), reads=[cR], writes=[cR])

            hT = S.sbuf([128, 2048], F32, "hT")
            hTR = Region()
            hS = [S.sbuf([128, 2048], F32, "hS%d" % b) for b in range(2)]
            hSR = [Region(), Region()]
            hb_ring = Ring([S.sbuf([128, 2048], BF16, "hTb%d" % i) for i in range(3)])
            S.op("pool", lambda e: e.memset(hT[:], 0.0), writes=[hTR])

            bank = Ring([S.psum([128, 512], F32, "sbk%d" % i) for i in range(7)])
            tbank = Ring([S.psum([128, 8, 128], BF16, "stb%d" % i) for i in range(1)])
            ldst = S.sbuf([128, 16, 128], F32, "ldst")
            ldR = Region()

            for b in range(2):
                S.dma("sp", ldst[:], sssm[b].rearrange("(cc p) n -> p cc n", p=128), writes=[ldR])
                for g in range(4):
                    ps, psR = bank.next()
                    for j in range(4):
                        S.op("pe", lambda e: e.transpose(ps[:, j * 128:(j + 1) * 128],
                                                         ldst[:, g * 4 + j, :], ident_f[:]),
                             reads=[ldR, constR], writes=[psR])
                    S.op("dve", lambda e: e.tensor_copy(out=hS[b][:, g * 512:(g + 1) * 512],
                                                        in_=ps[:]), reads=[psR], writes=[hSR[b]])

            def store_state(src, srcR, idx):
                for g in range(4):
                    ps, psR = bank.next()
                    for j in range(4):
                        cc = g * 4 + j
                        S.op("pe", lambda e: e.transpose(ps[:, j * 128:(j + 1) * 128],
                                                         src[:, cc * 128:(cc + 1) * 128], ident_f[:]),
                             reads=[srcR, constR], writes=[psR])
                    S.op("dve", lambda e: e.tensor_copy(
                        out=ldst[:, g * 4:(g + 1) * 4, :],
                        in_=ps[:].rearrange("p (a b) -> p a b", a=4)), reads=[psR], writes=[ldR])
                S.dma("sp", o_ssm[idx].rearrange("(cc p) n -> p cc n", p=128), ldst[:], reads=[ldR])

            win = Ring([S.sbuf([128, 24, 134], BF16, "win%d" % i) for i in range(2)])
            zin = Ring([S.sbuf([128, 2048], F32, "zin%d" % i) for i in range(2)])
            xsb = S.sbuf([128, 2560], BF16, "xsb"); xsbR = Region()
            BT = S.sbuf([128, 4, 128], BF16, "BT"); CT = S.sbuf([128, 4, 128], BF16, "CT")
            bcR = Region()
            sm = S.sbuf([128, 8, 32], F32, "sm"); smR = Region()
            Xt = S.sbuf([128, 32, 64], F32, "Xt"); XR = Region()
            Et = S.sbuf([128, 32, 64], F32, "Et"); ER = Region()
            CBm = S.sbuf([128, 4, 128], F32, "CBm"); CBR = Region()
            Wbd = S.sbuf([128, 32, 128], BF16, "Wbd"); WR = Region()
            xdt = S.sbuf([128, 2048], BF16, "xdt"); xdtR = Region()
            xw = S.sbuf([128, 2048], BF16, "xw"); xwR = Region()
            yt = S.sbuf([128, 2048], F32, "yt"); ytR = Region()
            tmp = Ring([S.sbuf([128, 512], F32, "stmp%d" % i) for i in range(2)])
            ssq = S.sbuf([128, 8], F32, "ssq"); ssqR = Region()
            junk = S.sbuf([128, 512], BF16, "junk"); junkR = Region()
            obt = S.sbuf([128, 2048], BF16, "obt"); obtR = Region()
            obT = Ring([S.sbuf([128, 16, 128], BF16, "obT%d" % i) for i in range(2)])
            with nc.allow_non_contiguous_dma(reason="3-row conv state halo"):
                pass

            for t in range(NT):
                samp = (t == 16)
                subs = [(0, 64, 0), (67, 64, 64)] if samp else [(0, 128, 0)]
                wt_, wR_ = win.next()
                srcv = xbcT.rearrange("c p w -> p c w")
                if samp:
                    for b in range(2):
                        S.dma("sp", wt_[:, :, 67 * b + 3:67 * b + 67],
                              srcv[:, :, 2048 + 64 * b:2048 + 64 * b + 64],
                              reads=[xbcT_R], writes=[wR_])
                        with nc.allow_non_contiguous_dma(reason="3-row conv state halo"):
                            for r in range(3):
                                S.dma("pool", wt_[:, :, 67 * b + r],
                                      sconv[b, r].rearrange("(cc p) -> p cc", p=128), writes=[wR_])
                elif t == 0:
                    S.dma("sp", wt_[:, :, 3:131], srcv[:, :, 0:128], reads=[xbcT_R], writes=[wR_])
                    S.op("pool", lambda e: e.memset(wt_[:, :, 0:3], 0.0), writes=[wR_])
                else:
                    S.dma("sp", wt_[:, :, 0:131], srcv[:, :, t * 128 - 3:t * 128 + 128],
                          reads=[xbcT_R], writes=[wR_])
                zt_, zR_ = zin.next()
                S.dma("sp", zt_[:], zs[t * 128:(t + 1) * 128, :], reads=[zs_R], writes=[zR_])

                for i in range(5):
                    ps, psR = bank.next()
                    for (c0, ntok, prow) in subs:
                        S.op("pe", lambda e: e.matmul(
                            ps[prow:prow + ntok, :], lhsT=ones_b[0:1, 0:ntok],
                            rhs=cb_row[0:1, i * 512:(i + 1) * 512], start=True, stop=False),
                            reads=[cR, constR], writes=[psR])
                        for c4 in range(4):
                            cc = i * 4 + c4
                            for j in range(4):
                                S.op("pe", lambda e: e.matmul(
                                    ps[prow:prow + ntok, c4 * 128:(c4 + 1) * 128],
                                    lhsT=wt_[:, cc, c0 + j:c0 + j + ntok], rhs=diagw[:, cc, j, :],
                                    start=False, stop=(c4 == 3 and j == 3)),
                                    reads=[wR_, cR], writes=[psR])
                    S.op("act", lambda e: e.activation(out=xsb[:, i * 512:(i + 1) * 512], in_=ps[:],
                                                       func=AF.Silu), reads=[psR], writes=[xsbR])
                for half, dstT in ((0, BT), (1, CT)):
                    ps, psR = bank.next()
                    for g in range(4):
                        cc = 16 + half * 4 + g
                        for (c0, ntok, prow) in subs:
                            for j in range(4):
                                S.op("pe", lambda e: e.matmul(
                                    ps[:, g * 128 + prow:g * 128 + prow + ntok],
                                    lhsT=diagw[:, cc, j, :], rhs=wt_[:, cc, c0 + j:c0 + j + ntok],
                                    start=(j == 0), stop=(j == 3)),
                                    reads=[wR_, cR], writes=[psR])
                    for g in range(4):
                        cc = 16 + half * 4 + g
                        S.op("act", lambda e: e.activation(
                            out=dstT[:, g, :], in_=ps[:, g * 128:(g + 1) * 128], func=AF.Silu,
                            bias=cbT[:, cc:cc + 1], scale=1.0), reads=[psR, cR], writes=[bcR])
                S.op("dve", lambda e: e.tensor_tensor(out=sm[:, 0, :], in0=dt_raw[:, t, :],
                                                      in1=dtb_bc[:], op=ALU.add),
                     reads=[dtR, cR], writes=[smR])
                S.op("act", lambda e: e.activation(out=sm[:, 0, :], in_=sm[:, 0, :], func=AF.Exp),
                     reads=[smR], writes=[smR])
                S.op("act", lambda e: e.activation(out=sm[:, 1, :], in_=sm[:, 0, :], func=AF.Ln,
                                                   bias=1.0, scale=1.0), reads=[smR], writes=[smR])
                S.op("dve", lambda e: e.tensor_tensor(out=sm[:, 2, :], in0=sm[:, 1, :], in1=a_bc[:],
                                                      op=ALU.mult), reads=[smR, cR], writes=[smR])
                ps, psR = bank.next()
                S.op("pe", lambda e: e.matmul(ps[:, 0:32], lhsT=Tcum[:], rhs=sm[:, 2, :], start=True,
                                              stop=True), reads=[smR, cR], writes=[psR])
                ps2, psR2 = bank.next()
                S.op("pe", lambda e: e.matmul(ps2[:, 0:32], lhsT=Bones[:], rhs=sm[:, 2, :],
                                              start=True, stop=True), reads=[smR, cR], writes=[psR2])
                ps3, psR3 = bank.next()
                for c in range(2):
                    S.op("pe", lambda e: e.matmul(ps3[:, c * 32:(c + 1) * 32], lhsT=Cones[:, c, :],
                                                  rhs=sm[:, 2, :], start=True, stop=True),
                         reads=[smR, cR], writes=[psR3])
                S.op("dve", lambda e: e.tensor_copy(out=sm[:, 3, :], in_=ps[:, 0:32]),
                     reads=[psR], writes=[smR])
                S.op("act", lambda e: e.activation(out=sm[:, 4, :], in_=ps[:, 0:32], func=AF.Exp),
                     reads=[psR], writes=[smR])
                S.op("dve", lambda e: e.tensor_tensor(out=sm[:, 5, :], in0=ps2[:, 0:32],
                                                      in1=sm[:, 3, :], op=ALU.subtract),
                     reads=[psR2, smR], writes=[smR])
                S.op("act", lambda e: e.activation(out=sm[:, 5, :], in_=sm[:, 5, :], func=AF.Exp),
                     reads=[smR], writes=[smR])
                S.op("dve", lambda e: e.tensor_tensor(out=sm[:, 5, :], in0=sm[:, 5, :],
                                                      in1=sm[:, 1, :], op=ALU.mult),
                     reads=[smR], writes=[smR])
                S.op("act", lambda e: e.activation(
                    out=sm[:, 6:8, :], in_=ps3[:, 0:64].rearrange("p (c h) -> p c h", c=2),
                    func=AF.Exp), reads=[psR3], writes=[smR])
                xs3 = xsb[:, 0:2048].rearrange("p (h d) -> p h d", h=32)
                S.op("dve", lambda e: e.tensor_tensor(
                    out=xdt[:].rearrange("p (h d) -> p h d", h=32), in0=xs3,
                    in1=sm[:, 1, :].unsqueeze(2).broadcast_to([128, 32, 64]), op=ALU.mult),
                    reads=[xsbR, smR], writes=[xdtR])
                S.op("dve", lambda e: e.tensor_tensor(
                    out=xw[:].rearrange("p (h d) -> p h d", h=32), in0=xs3,
                    in1=sm[:, 5, :].unsqueeze(2).broadcast_to([128, 32, 64]), op=ALU.mult),
                    reads=[xsbR, smR], writes=[xwR])
                S.op("dve", lambda e: e.tensor_tensor(
                    out=Xt[:], in0=sm[:, 2, :].unsqueeze(2).broadcast_to([128, 32, 64]),
                    in1=LT[:].unsqueeze(1).broadcast_to([128, 32, 64]), op=ALU.mult),
                    reads=[smR, cR], writes=[XR])
                for q4 in range(4):
                    ps, psR = bank.next()
                    S.op("pe", lambda e: e.matmul(
                        ps[:], lhsT=Mtri[:], rhs=Xt[:, q4 * 8:(q4 + 1) * 8, :], start=True, stop=True),
                        reads=[XR, cR], writes=[psR])
                    S.op("act", lambda e: e.activation(
                        out=Et[:, q4 * 8:(q4 + 1) * 8, :],
                        in_=ps[:].rearrange("p (h d) -> p h d", h=8), func=AF.Exp),
                        reads=[psR], writes=[ER])
                ps, psR = bank.next()
                for g in range(4):
                    S.op("pe", lambda e: e.matmul(ps[:, g * 128:(g + 1) * 128], lhsT=BT[:, g, :],
                                                  rhs=CT[:, g, :], start=True, stop=True),
                         reads=[bcR], writes=[psR])
                S.op("dve", lambda e: e.tensor_tensor(
                    out=CBm[:], in0=ps[:].rearrange("p (g t) -> p g t", g=4),
                    in1=BDm[:].unsqueeze(1).broadcast_to([128, 4, 128]), op=ALU.mult),
                    reads=[psR, cR], writes=[CBR])
                for g in range(4):
                    S.op("dve", lambda e: e.tensor_tensor(
                        out=Wbd[:, g * 8:(g + 1) * 8, :].rearrange("p h (c t) -> p h c t", c=2),
                        in0=Et[:, g * 8:(g + 1) * 8, :].unsqueeze(2).broadcast_to([128, 8, 2, 64]),
                        in1=CBm[:, g, :].rearrange("p (c t) -> p c t", c=2).unsqueeze(1).broadcast_to(
                            [128, 8, 2, 64]), op=ALU.mult),
                        reads=[ER, CBR], writes=[WR])
                hin = []
                for c in range(2):
                    if samp:
                        hin.append((hS[c], hSR[c]))
                    else:
                        hin.append((hT, hTR))
                hb0, hb0R = hb_ring.next()
                S.op("act", lambda e: e.copy(out=hb0[:], in_=hin[0][0][:]), reads=[hin[0][1]],
                     writes=[hb0R])
                hbs = [(hb0, hb0R), None]
                ypsl = []
                for g in range(4):
                    ps, psR = bank.next()
                    for hh in range(8):
                        h = g * 8 + hh
                        S.op("pe", lambda e: e.matmul(ps[:, hh * 64:(hh + 1) * 64], lhsT=Wbd[:, h, :],
                                                      rhs=xdt[:, h * 64:(h + 1) * 64], start=True,
                                                      stop=False),
                             reads=[WR, xdtR], writes=[psR])
                        S.op("pe", lambda e: e.matmul(ps[:, hh * 64:(hh + 1) * 64], lhsT=Dd[:, h, :],
                                                      rhs=xsb[:, h * 64:(h + 1) * 64], start=False,
                                                      stop=True),
                             reads=[cR, xsbR], writes=[psR])
                    S.op("dve", lambda e: e.tensor_copy(out=yt[:, g * 512:(g + 1) * 512], in_=ps[:]),
                         reads=[psR], writes=[ytR])
                for c in range(2):
                    cs = slice(c * 64, (c + 1) * 64)
                    hsrc, hsrcR = hin[c]
                    if c == 1:
                        if samp:
                            hb1, hb1R = hb_ring.next()
                            S.op("act", lambda e: e.copy(out=hb1[:], in_=hsrc[:]), reads=[hsrcR],
                                 writes=[hb1R])
                        else:
                            hb1, hb1R = hb_ring.next()
                            S.op("act", lambda e: e.copy(out=hb1[:], in_=hT[:]), reads=[hTR],
                                 writes=[hb1R])
                        hbs[1] = (hb1, hb1R)
                    hb, hbR = hbs[c]
                    for g in range(4):
                        gs = slice(g * 512, (g + 1) * 512)
                        ps, psR = bank.next()
                        S.op("pe", lambda e: e.matmul(ps[cs, :], lhsT=CT[:, g, cs], rhs=hb[:, gs],
                                                      start=True, stop=True),
                             reads=[bcR, hbR], writes=[psR])
                        tm, tmR = tmp.next()
                        S.op("dve", lambda e: e.tensor_tensor(
                            out=tm[cs, :].rearrange("p (h d) -> p h d", h=8),
                            in0=ps[cs, :].rearrange("p (h d) -> p h d", h=8),
                            in1=sm[cs, 4, g * 8:(g + 1) * 8].unsqueeze(2).broadcast_to([64, 8, 64]),
                            op=ALU.mult), reads=[psR, smR], writes=[tmR])
                        S.op("dve", lambda e: e.tensor_tensor(out=yt[cs, gs], in0=yt[cs, gs],
                                                              in1=tm[cs, :], op=ALU.add),
                             reads=[tmR, ytR], writes=[ytR])
                        ps, psR = bank.next()
                        S.op("pe", lambda e: e.matmul(
                            ps[:], lhsT=xsb[cs, 2048 + g * 128:2048 + (g + 1) * 128], rhs=xw[cs, gs],
                            start=True, stop=True), reads=[xsbR, xwR], writes=[psR])
                        tm, tmR = tmp.next()
                        S.op("pool", lambda e: e.tensor_tensor(
                            out=tm[:].rearrange("p (h d) -> p h d", h=8),
                            in0=hsrc[:, gs].rearrange("p (h d) -> p h d", h=8),
                            in1=sm[:, 6 + c, g * 8:(g + 1) * 8].unsqueeze(2).broadcast_to(
                                [128, 8, 64]), op=ALU.mult),
                            reads=[hsrcR, smR, hbR], writes=[tmR])
                        S.op("dve", lambda e: e.tensor_tensor(out=hsrc[:, gs], in0=tm[:], in1=ps[:],
                                                              op=ALU.add),
                             reads=[tmR, psR, hbR], writes=[hsrcR])
                S.op("dve", lambda e: e.tensor_tensor(out=yt[:], in0=yt[:], in1=zt_[:], op=ALU.mult),
                     reads=[ytR, zR_], writes=[ytR])
                for g in range(4):
                    S.op("act", lambda e: e.activation(
                        out=junk[:], in_=yt[:, g * 512:(g + 1) * 512], func=AF.Square,
                        accum_out=ssq[:, g:g + 1]), reads=[ytR], writes=[junkR, ssqR])
                S.op("dve", lambda e: e.tensor_scalar(
                    out=ssq[:, 4:8], in0=ssq[:, 0:4], scalar1=1.0 / 512.0, scalar2=1e-5,
                    op0=ALU.mult, op1=ALU.add), reads=[ssqR], writes=[ssqR])
                S.op("act", lambda e: e.activation(out=ssq[:, 4:8], in_=ssq[:, 4:8], func=AF.Ln),
                     reads=[ssqR], writes=[ssqR])
                S.op("act", lambda e: e.activation(out=ssq[:, 4:8], in_=ssq[:, 4:8], func=AF.Exp,
                                                   scale=-0.5), reads=[ssqR], writes=[ssqR])
                for g in range(4):
                    gs = slice(g * 512, (g + 1) * 512)
                    S.op("dve", lambda e: e.scalar_tensor_tensor(
                        out=obt[:, gs], in0=yt[:, gs], scalar=ssq[:, 4 + g:5 + g], in1=nw_bc[:, gs],
                        op0=ALU.mult, op1=ALU.mult), reads=[ytR, ssqR, cR], writes=[obtR])
                oT_, oTR_ = obT.next()
                for g2 in range(2):
                    pt, ptR = tbank.next()
                    for j in range(8):
                        cc = g2 * 8 + j
                        S.op("pe", lambda e: e.transpose(pt[:, j, :], obt[:, cc * 128:(cc + 1) * 128],
                                                         ident_b[:]), reads=[obtR, constR], writes=[ptR])
                    S.op("act", lambda e: e.copy(out=oT_[:, g2 * 8:(g2 + 1) * 8, :], in_=pt[:]),
                         reads=[ptR], writes=[oTR_])
                S.dma("sp", oT0[8:24, :, t * 128:(t + 1) * 128].rearrange("c p w -> p c w"), oT_[:],
                      reads=[oTR_], writes=oT0_R[8:24])
                if t == 15:
                    store_state(hT, hTR, 0)
            for b in range(2):
                store_state(hS[b], hSR[b], 1 + b)


    ALPHA = (2 * 2) ** 0.25

    def outproj_ln(oT, oT_R, ncch, w_out, ln_g, ln_b, src_p, src_s, srcR, dst_p, dst_s, dstR):
        with S.scope():
            cR = Region()
            wo = S.sbuf([128, ncch, 2048], BF16, "wo")
            wov = w_out.rearrange("(c p) n -> p c n", p=128)
            for c0 in range(0, ncch, 4):
                S.dma("pool", wo[:, c0:c0 + 4, :], wov[:, c0:c0 + 4, :], writes=[cR])
            g_bc = S.sbuf([128, 2048], F32, "g_bc")
            b_bc = S.sbuf([128, 2048], F32, "b_bc")
            S.dma("sp", g_bc[:], ln_g.partition_broadcast(128), writes=[cR])
            S.dma("sp", b_bc[:], ln_b.partition_broadcast(128), writes=[cR])
            oTt = Ring([S.sbuf([128, ncch, 128], BF16, "oTt%d" % i) for i in range(2)])
            xr = Ring([S.sbuf([128, 2048], F32, "xr%d" % i) for i in range(2)])
            pre = Ring([S.sbuf([128, 2048], F32, "pre%d" % i) for i in range(2)])
            yo = Ring([S.sbuf([128, 2048], F32, "yo%d" % i) for i in range(2)])
            st = Ring([S.sbuf([128, 16], F32, "lnst%d" % i) for i in range(2)])
            junk = S.sbuf([128, 2048], BF16, "lnjunk")
            junkR = Region()
            bank = Ring([S.psum([128, 512], F32, "opb%d" % i) for i in range(8)])
            oTv = oT.rearrange("c p w -> p c w")
            for t in range(NT):
                ot, otR = oTt.next()
                S.dma("sp", ot[:], oTv[:, :, t * 128:(t + 1) * 128], reads=oT_R, writes=[otR])
                xt_, xR_ = xr.next()
                S.dma("sp", xt_[:], x_rows(src_p, src_s, t), reads=[srcR], writes=[xR_])
                pt_, pR_ = pre.next()
                st_, sR_ = st.next()
                for nb in range(4):
                    ps, psR = bank.next()
                    for c in range(ncch):
                        S.op("pe", lambda e: e.matmul(ps[:], lhsT=ot[:, c, :],
                                                      rhs=wo[:, c, nb * 512:(nb + 1) * 512],
                                                      start=(c == 0), stop=(c == ncch - 1)),
                             reads=[otR, cR], writes=[psR])
                    cs = slice(nb * 512, (nb + 1) * 512)
                    S.op("dve", lambda e: e.scalar_tensor_tensor(
                        out=pt_[:, cs], in0=xt_[:, cs], scalar=ALPHA, in1=ps[:], op0=ALU.mult,
                        op1=ALU.add, accum_out=st_[:, nb:nb + 1]),
                        reads=[xR_, psR], writes=[pR_, sR_])
                S.op("act", lambda e: e.activation(out=junk[:], in_=pt_[:], func=AF.Square,
                                                   accum_out=st_[:, 4:5]),
                     reads=[pR_], writes=[junkR, sR_])
                S.op("dve", lambda e: e.tensor_reduce(out=st_[:, 5:6], in_=st_[:, 0:4], axis=AX.X,
                                                      op=ALU.add), reads=[sR_], writes=[sR_])
                S.op("dve", lambda e: e.tensor_scalar(out=st_[:, 6:7], in0=st_[:, 5:6],
                                                      scalar1=1.0 / 2048.0, scalar2=None,
                                                      op0=ALU.mult), reads=[sR_], writes=[sR_])
                S.op("dve", lambda e: e.tensor_tensor(out=st_[:, 7:8], in0=st_[:, 6:7],
                                                      in1=st_[:, 6:7], op=ALU.mult),
                     reads=[sR_], writes=[sR_])
                S.op("dve", lambda e: e.scalar_tensor_tensor(
                    out=st_[:, 8:9], in0=st_[:, 4:5], scalar=1.0 / 2048.0, in1=st_[:, 7:8],
                    op0=ALU.mult, op1=ALU.subtract), reads=[sR_], writes=[sR_])
                S.op("dve", lambda e: e.tensor_scalar(out=st_[:, 8:9], in0=st_[:, 8:9], scalar1=1e-5,
                                                      scalar2=None, op0=ALU.add),
                     reads=[sR_], writes=[sR_])
                S.op("act", lambda e: e.activation(out=st_[:, 9:10], in_=st_[:, 8:9], func=AF.Ln),
                     reads=[sR_], writes=[sR_])
                S.op("act", lambda e: e.activation(out=st_[:, 9:10], in_=st_[:, 9:10], func=AF.Exp,
                                                   scale=-0.5), reads=[sR_], writes=[sR_])
                S.op("dve", lambda e: e.scalar_tensor_tensor(
                    out=st_[:, 10:11], in0=st_[:, 6:7], scalar=-1.0, in1=st_[:, 9:10],
                    op0=ALU.mult, op1=ALU.mult), reads=[sR_], writes=[sR_])
                yt_, yR_ = yo.next()
                S.op("act", lambda e: e.activation(out=yt_[:], in_=pt_[:], func=AF.Identity,
                                                   scale=st_[:, 9:10], bias=st_[:, 10:11]),
                     reads=[pR_, sR_], writes=[yR_])
                S.op("dve", lambda e: e.tensor_tensor(out=yt_[:], in0=yt_[:], in1=g_bc[:],
                                                      op=ALU.mult), reads=[yR_, cR], writes=[yR_])
                S.op("dve", lambda e: e.tensor_tensor(out=yt_[:], in0=yt_[:], in1=b_bc[:],
                                                      op=ALU.add), reads=[yR_, cR], writes=[yR_])
                S.dma("sp", x_rows(dst_p, dst_s, t), yt_[:], reads=[yR_], writes=[dstR])

    oT1 = dscr("oT1", [16, 128, TOK], BF16)
    oT1_R = [Region() for _ in range(16)]
    y0 = dscr("y0", [TOK, 2048], F32)
    y0R = Region()
    Gd = dscr("Gd", [32, 768], F32)
    GdR = Region()

    def layer1_attn():
        W1v = w_in1.rearrange("(kc p) n -> p kc n", p=128)
        with S.scope():
            cR = Region()
            wA = Ring([S.sbuf([128, KC, 4, 128], BF16, "w1A%d" % i) for i in range(2)])
            kv_st = Ring([S.sbuf([128, 4, 256], F32, "kv1st%d" % i) for i in range(2)])
            qTs = S.sbuf([128, TOK], BF16, "q1Ts")
            kT = S.sbuf([128, TOK], BF16, "k1T")
            sgT = S.sbuf([128, TOK], BF16, "sg1T")
            v_p = S.sbuf([128, NT, 128], BF16, "v1_p")
            qR, kR, gR, vR = Region(), Region(), Region(), Region()
            kst = S.sbuf([128, 4, 128], F32, "k1st")
            kstR = Region()
            kTp = S.sbuf([128, 2, 640], BF16, "k1Tp")
            kTpR = Region()
            vpast = S.sbuf([128, 2, 5, 128], BF16, "v1past")
            vpR = Region()
            S.op("pool", lambda e: e.memset(kTp[:, :, 512:640], 0.0), writes=[kTpR])
            S.op("pool", lambda e: e.memset(vpast[:, :, 4, :], 0.0), writes=[vpR])
            rbt = S.sbuf([32, 257], F32, "rbt")
            Gs = S.sbuf([32, 768], F32, "Gs")
            S.dma("sp", rbt[:], rel_bias[:, :], writes=[cR])
            S.op("dve", lambda e: e.tensor_copy(out=Gs[:, 0:256], in_=rbt[:, 1:257]),
                 reads=[cR], writes=[cR])
            S.op("dve", lambda e: e.tensor_copy(out=Gs[:, 256:768],
                                                in_=rbt[:, 256:257].broadcast_to([32, 512])),
                 reads=[cR], writes=[cR])
            S.dma("sp", Gd[:, :], Gs[:], reads=[cR], writes=[GdR])
            Jm = S.sbuf([128, 128], F32, "Jm")
            S.op("pool", lambda e: e.memset(Jm[:], 0.0), writes=[cR])
            S.op("pool", lambda e: e.affine_select(
                out=Jm[:], in_=Jm[:], pattern=[[1, 128]], compare_op=ALU.not_equal, fill=1.0,
                base=-127, channel_multiplier=1), reads=[cR], writes=[cR])
            T1 = S.sbuf([128, 640], F32, "T1")
            T1R = Region()
            eBT = S.sbuf([128, 2, 640], F32, "eBT")
            eBR = Region()
            st_ring = Ring([S.psum([128, 512], F32, "b_st%d" % i) for i in range(2)])
            od_ring = Ring([S.psum([128, 512], F32, "b_od%d" % i) for i in range(4)])
            pproj = Ring([S.psum([128, 512], F32, "b_pp%d" % i) for i in range(2)])
            wraw = Ring([S.sbuf([128, 512], F32, "wraw%d" % i) for i in range(3)])
            wbf = Ring([S.sbuf([128, 512], BF16, "wbf%d" % i) for i in range(4)])
            rec = Ring([S.sbuf([128, 512], F32, "rec%d" % i) for i in range(2)])
            ogs = Ring([S.sbuf([128, 512], BF16, "og1%d" % i) for i in range(2)])
            kvR_all = Region()
            dmy = S.sbuf([128, 8], F32, "dmy1_sync")

            def load_wA(p):
                wt, wR = wA.next()
                for s in range(4):
                    c0 = s * 2048 + p * 128
                    S.dma("pool", wt[:, :, s, :], W1v[:, :, c0:c0 + 128], writes=[wR])
                return wt, wR

            def proj(p, wt, wR):
                for s in (0, 1, 3):
                    for (t0, n) in TG:
                        ps, psR = pproj.next()
                        for kc in range(KC):
                            S.op("pe", lambda e: e.matmul(
                                ps[:, 0:n], lhsT=wt[:, kc, s, :], rhs=xT[:, kc, t0:t0 + n],
                                start=(kc == 0), stop=(kc == KC - 1)),
                                reads=[wR] + xT_R[t0 // 128:(t0 + n) // 128], writes=[psR])
                        if s == 0:
                            S.op("dve", lambda e: e.tensor_scalar(
                                out=qTs[:, t0:t0 + n], in0=ps[:, 0:n], scalar1=0.125, scalar2=None,
                                op0=ALU.mult), reads=[psR], writes=[qR])
                        elif s == 1:
                            S.op("dve", lambda e: e.tensor_copy(out=kT[:, t0:t0 + n],
                                                                in_=ps[:, 0:n]),
                                 reads=[psR], writes=[kR])
                        else:
                            S.op("act", lambda e: e.activation(out=sgT[:, t0:t0 + n],
                                                               in_=ps[:, 0:n], func=AF.Silu),
                                 reads=[psR], writes=[gR])
                for t0 in range(0, NT, 4):
                    tl = list(range(t0, min(t0 + 4, NT)))
                    stg, stR = kv_st.next()
                    for i, t in enumerate(tl):
                        ps, psR = pproj.next()
                        for kc in range(KC):
                            S.op("pe", lambda e: e.matmul(
                                ps[:, 0:256], lhsT=xT[:, kc, t * 128:(t + 1) * 128],
                                rhs=wt[:, kc, 1:3, :], start=(kc == 0), stop=(kc == KC - 1)),
                                reads=[xT_R[t], wR], writes=[psR])
                        S.op("dve", lambda e: e.tensor_copy(out=stg[:, i, :], in_=ps[:, 0:256]),
                             reads=[psR], writes=[stR])
                        S.op("pool", lambda e: e.tensor_copy(out=v_p[:, t, :],
                                                             in_=stg[:, i, 128:256]),
                             reads=[stR], writes=[vR])
                    cs = slice(p * 128, (p + 1) * 128)
                    if t0 == 12:
                        for s_, dst in ((0, o_bk_p), (1, o_bv_p)):
                            S.dma("sp", dst[:, cs].rearrange("(t q) c -> q t c", q=128),
                                  stg[:, 0:4, s_ * 128:(s_ + 1) * 128], reads=[stR])
                    if 16 in tl:
                        i = tl.index(16)
                        for b in range(2):
                            for s_, dst in ((0, o_bk_s), (1, o_bv_s)):
                                S.dma("sp", dst[b, 448:512, cs],
                                      stg[64 * b:64 * b + 64, i, s_ * 128:(s_ + 1) * 128],
                                      reads=[stR])

            def load_sample_kv(p):
                cs = slice(p * 128, (p + 1) * 128)
                for b in range(2):
                    S.dma("pool", vpast[:, b, 0:4, :],
                          cbv[b].rearrange("(blk q) c -> q blk c", q=128)[:, :, cs], writes=[vpR])
                    S.dma("sp", kst[:], cbk[b].rearrange("(blk q) c -> q blk c", q=128)[:, :, cs],
                          writes=[kstR])
                    ps, psR = pproj.next()
                    for j in range(4):
                        S.op("pe", lambda e: e.transpose(ps[:, j * 128:(j + 1) * 128], kst[:, j, :],
                                                         ident_f[:]),
                             reads=[kstR, constR], writes=[psR])
                    S.op("dve", lambda e: e.tensor_copy(out=kTp[:, b, 0:512], in_=ps[:]),
                         reads=[psR], writes=[kTpR])
                for b in range(2):
                    S.op("pool", lambda e: e.tensor_copy(
                        out=kTp[:, b, 512:576], in_=kT[:, 2048 + 64 * b:2048 + 64 * b + 64]),
                        reads=[kR], writes=[kTpR])
                    S.dma("sp", vpast[0:64, b, 4, :], v_p[64 * b:64 * b + 64, 16, :],
                          reads=[vR], writes=[vpR])

            def bias_tiles(p):
                for hh in range(2):
                    h = 2 * p + hh
                    src = bass.AP(tensor=Gd.tensor, offset=Gd[h, 0:1].offset, ap=[[1, 128], [1, 640]])
                    S.dma("sp", T1[:], src, reads=[GdR], writes=[T1R])
                    for (c0, n) in ((0, 512), (512, 128)):
                        ps, psR = pproj.next()
                        S.op("pe", lambda e: e.matmul(ps[:, 0:n], lhsT=Jm[:], rhs=T1[:, c0:c0 + n],
                                                      start=True, stop=True),
                             reads=[T1R, cR], writes=[psR])
                        S.op("act", lambda e: e.activation(out=eBT[:, hh, c0:c0 + n], in_=ps[:, 0:n],
                                                           func=AF.Exp), reads=[psR], writes=[eBR])

            def band_block(hs, hh, kap, vap, qap, ncol, bcol0, zero_lo, zero_hi, oacc, dacc, odR,
                           ddR, ocols, first, last):
                box = {}

                def s0():
                    sps, spR = st_ring.next()
                    S.op("pe", lambda e: e.matmul(sps[:, 0:ncol], lhsT=kap, rhs=qap, start=True,
                                                  stop=True), reads=[kvR_all, qR], writes=[spR])
                    box["sps"] = (sps, spR)

                def s1():
                    sps, spR = box["sps"]
                    wr, wrR = wraw.next()
                    S.op("act", lambda e: e.activation(out=wr[:, 0:ncol], in_=sps[:, 0:ncol],
                                                       func=AF.Exp), reads=[spR], writes=[wrR])
                    box["wr"] = (wr, wrR)

                def s2():
                    wr, wrR = box["wr"]
                    wb, wbR = wbf.next()
                    S.op("dve", lambda e: e.tensor_tensor(out=wb[:, 0:ncol], in0=wr[:, 0:ncol],
                                                          in1=eBT[:, hh, bcol0:bcol0 + ncol],
                                                          op=ALU.mult),
                         reads=[wrR, eBR], writes=[wbR])
                    if zero_lo is not None:
                        S.op("pool", lambda e: e.memset(wb[64:128, zero_lo:zero_lo + 64], 0.0),
                             reads=[wbR], writes=[wbR])
                    if zero_hi is not None:
                        S.op("pool", lambda e: e.memset(wb[0:64, zero_hi:zero_hi + 64], 0.0),
                             reads=[wbR], writes=[wbR])
                    box["wb"] = (wb, wbR)

                def s3():
                    wb, wbR = box["wb"]
                    S.op("pe", lambda e: e.matmul(oacc[hs, ocols], lhsT=vap, rhs=wb[:, 0:ncol],
                                                  start=first, stop=last, skip_group_check=True),
                         reads=[kvR_all, wbR], writes=[odR])
                    S.op("pe", lambda e: e.matmul(dacc[hs, ocols], lhsT=ones_b[:, 0:64],
                                                  rhs=wb[:, 0:ncol], start=first, stop=last,
                                                  skip_group_check=True),
                         reads=[constR, wbR], writes=[ddR])
                return (s0, s1, s2, s3)

            def run_pipe(stages, depth=2):
                n = len(stages)
                for step in range(n + 3):
                    for lag in (3, 2, 1, 0):
                        idx = step - lag
                        if 0 <= idx < n:
                            stages[idx][lag]()

            def finish_group(p, oacc, dacc, odR, ddR, ncols, tok0):
                rc, rcR = rec.next()
                S.op("dve", lambda e: e.reciprocal(out=rc[:, 0:ncols], in_=dacc[:, 0:ncols]),
                     reads=[ddR], writes=[rcR])
                S.op("dve", lambda e: e.tensor_tensor(out=rc[:, 0:ncols], in0=oacc[:, 0:ncols],
                                                      in1=rc[:, 0:ncols], op=ALU.mult),
                     reads=[odR, rcR], writes=[rcR])
                og, ogR = ogs.next()
                S.op("pool", lambda e: e.tensor_tensor(out=og[:, 0:ncols], in0=rc[:, 0:ncols],
                                                       in1=sgT[:, tok0:tok0 + ncols], op=ALU.mult),
                     reads=[rcR, gR], writes=[ogR])
                S.dma("sp", oT1[p, :, tok0:tok0 + ncols], og[:, 0:ncols], reads=[ogR],
                      writes=[oT1_R[p]])

            def attn(p):
                for G in range(4):
                    oacc, oR_ = od_ring.next()
                    dacc, dR_ = od_ring.next()
                    odR = oR_
                    stages = []
                    for hh in range(2):
                        hs = slice(hh * 64, hh * 64 + 64)
                        ms = list(range(max(0, 4 * G - 4), 4 * G + 4))
                        for mi, m in enumerate(ms):
                            cmin = max(8 * G, 2 * m)
                            cmax = min(8 * G + 7, 2 * m + 9)
                            ncol = (cmax - cmin + 1) * 64
                            oc0 = (cmin - 8 * G) * 64
                            zero_lo = 0 if cmin == 2 * m else None
                            zero_hi = (ncol - 64) if cmax == 2 * m + 9 else None
                            stages.append(band_block(
                                hs, hh, kT[hs, m * 128:(m + 1) * 128], v_p[:, m, hs],
                                qTs[hs, cmin * 64:cmin * 64 + ncol], ncol,
                                64 * (cmin - 2 * m), zero_lo, zero_hi, oacc, dacc, odR,
                                dR_, slice(oc0, oc0 + ncol), mi == 0, mi == len(ms) - 1))
                    run_pipe(stages)
                    finish_group(p, oacc, dacc, odR, dR_, 512, G * 512)
                oacc, oR_ = od_ring.next()
                dacc, dR_ = od_ring.next()
                odR = oR_
                stages = []
                for hh in range(2):
                    hs = slice(hh * 64, hh * 64 + 64)
                    for b in range(2):
                        tq = slice(2048 + 64 * b, 2048 + 64 * b + 64)
                        for j in range(5):
                            bcol0 = 0 if j == 4 else 512 - 128 * j
                            stages.append(band_block(
                                hs, hh, kTp[hs, b, j * 128:(j + 1) * 128], vpast[:, b, j, hs],
                                qTs[hs, tq], 64, bcol0, (0 if j == 4 else None), None, oacc,
                                dacc, odR, dR_, slice(b * 64, b * 64 + 64),
                                (j == 0 and b == 0), j == 4))
                run_pipe(stages)
                finish_group(p, oacc, dacc, odR, dR_, 128, 2048)

            for b in range(2):
                S.dma("sp", o_bk_s[b, 0:448, :], cbk[b, 64:512, :])
                S.dma("sp", o_bv_s[b, 0:448, :], cbv[b, 64:512, :])

            npairs = NPAIRS_DBG if DEBUG else 16
            nxt = load_wA(0)
            for p in range(npairs):
                wt, wR = nxt
                if p + 1 < npairs:
                    nxt = load_wA(p + 1)
                proj(p, wt, wR)
                load_sample_kv(p)
                bias_tiles(p)
                S.op("pool", lambda e: e.memset(dmy[:, 0:1], 0.0),
                     reads=[kR, vR, kTpR, vpR], writes=[kvR_all])
                attn(p)
                S.op("pool", lambda e: e.memset(dmy[:, 0:1], 0.0),
                     reads=[kvR_all], writes=[kR, vR, kTpR, vpR, qR, gR])


    with S.scope():
        xT.t = S.sbuf([128, KC, TOK], BF16, "xT_l0")
        build_xT(xp, xs)
        if not SKIP_SB:
            layer0_sb()
        layer0_ssd_proj()
    ssd_phase()
    if not SKIP_REST:
        outproj_ln(oT0, oT0_R, 24, w_out0, ln_g0, ln_b0, xp, xs, Region(), y0[0:2048, :],
                   y0[2048:2176, :], y0R)
        with S.scope():
            xT.t = S.sbuf([128, KC, TOK], BF16, "xT_l1")
            for r_ in xT_R:
                r_.w = None
                r_.r = {}
            build_xT(y0[0:2048, :], y0[2048:2176, :], srcR=y0R)
            layer1_attn()
        outR = Region()
        outproj_ln(oT1, oT1_R, 16, w_out1, ln_g1, ln_b1, y0[0:2048, :], y0[2048:2176, :], y0R,
                   o_yp, o_ys, outR)

    S.barrier(engines=("sp",))
    print("program: %d instructions, %d waits" % (S.ninst, S.nwait))
    return nc, st


NPAIRS_DBG = 1


def kernel(x_prompt, x_sample, cache_sb_k, cache_sb_v, state_ssm, state_conv, cache_band_k,
           cache_band_v, even_w_in, even_conv_w, even_conv_b, even_dt_bias, even_a_log,
           even_d_skip, even_norm_w, even_w_out, even_ln_g, even_ln_b, odd_w_in, odd_rel_bias,
           odd_w_out, odd_ln_g, odd_ln_b, _debug_cores=None):
    f = lambda a: np.ascontiguousarray(np.asarray(a, dtype=np.float32))
    x_prompt = f(x_prompt)
    x_sample = f(x_sample)
    cache_sb_k = np.asarray(cache_sb_k)
    cache_sb_v = np.asarray(cache_sb_v)
    nc, st = build_program()
    in_maps = []
    for c in range(NCORES):
        in_maps.append({
            "xp": x_prompt[c],
            "xs": x_sample[2 * c:2 * c + 2].reshape(128, D),
            "csk": f(cache_sb_k[0, 2 * c:2 * c + 2]).reshape(2, 4096, 1024),
            "csv": f(cache_sb_v[0, 2 * c:2 * c + 2]).reshape(2, 4096, 1024),
            "w_in0": f(even_w_in[0]),
            "sssm": f(np.asarray(state_ssm)[0, 2 * c:2 * c + 2]).reshape(2, 2048, 128),
            "sconv": f(np.asarray(state_conv)[0, 2 * c:2 * c + 2]),
            "conv_w": f(even_conv_w[0]), "conv_b": f(even_conv_b[0]),
            "dt_bias": f(even_dt_bias[0]), "a_log": f(even_a_log[0]),
            "d_skip": f(even_d_skip[0]), "norm_w": f(even_norm_w[0]),
            "w_out0": f(even_w_out[0]), "ln_g0": f(even_ln_g[0]), "ln_b0": f(even_ln_b[0]),
            "cbk": f(np.asarray(cache_band_k)[0, 2 * c:2 * c + 2]).reshape(2, 512, 2048),
            "cbv": f(np.asarray(cache_band_v)[0, 2 * c:2 * c + 2]).reshape(2, 512, 2048),
            "w_in1": f(odd_w_in[0]), "rel_bias": f(odd_rel_bias[0]), "w_out1": f(odd_w_out[0]),
            "ln_g1": f(odd_ln_g[0]), "ln_b1": f(odd_ln_b[0]),
        })
    res = run_bass_kernel_spmd(nc, in_maps, core_ids=list(range(NCORES)))
    st.close()
    R = res.results
    if DEBUG:
        return R
    cat = lambda name: np.stack([np.asarray(R[c][name]) for c in range(NCORES)])
    B = 8
    y_p = cat("o_yp")
    y_s = cat("o_ys").reshape(16, 64, D)
    sbk_p = cat("o_sbk_p").reshape(1, B, SEQ, 16, 64)
    sbv_p = cat("o_sbv_p").reshape(1, B, SEQ, 16, 64)
    sbk_s = cat("o_sbk_s").reshape(1, 16, 64, 16, 64)
    sbv_s = cat("o_sbv_s").reshape(1, 16, 64, 16, 64)
    ssm = cat("o_ssm")
    ssm_p = ssm[:, 0].reshape(1, B, 32, 64, 128)
    ssm_s = ssm[:, 1:3].reshape(1, 16, 32, 64, 128)
    cv = cat("o_conv")
    conv_p = cv[:, 0:3].reshape(1, B, 3, 3072)
    conv_s = cv[:, 3:9].reshape(1, 16, 3, 3072)
    bk_p = cat("o_bk_p").reshape(1, B, 512, 32, 64)
    bv_p = cat("o_bv_p").reshape(1, B, 512, 32, 64)
    bk_s = cat("o_bk_s").reshape(1, 16, 512, 32, 64)
    bv_s = cat("o_bv_s").reshape(1, 16, 512, 32, 64)
    return (y_p, y_s, sbk_p, sbv_p, ssm_p, conv_p, bk_p, bv_p,
            sbk_s, sbv_s, ssm_s, conv_s, bk_s, bv_s)
```
